# Optimizing a Trainium2 kernel written in Bass

```python
import math
import jax, jax.numpy as jnp
from jax import lax
import numpy as np

D_MODEL = 1024
BATCH = 4
SEQ = 8192
DEPTH = 2

N_EVEN = (DEPTH + 1) // 2
N_ODD = DEPTH // 2
HEAD_DIM = 64
CONV_GROUPS = 8
CONV_WIDTH = CONV_GROUPS * HEAD_DIM
ATTN_HEADS = 8
ATTN_WIDTH = ATTN_HEADS * HEAD_DIM
MIX_IN = 3 * CONV_WIDTH + 3 * ATTN_WIDTH
MIX_OUT = CONV_WIDTH + ATTN_WIDTH
SHORT_CONV_K = 3
DILATED_PAIRS = ((128, 1), (512, 4), (2048, 16))
REL_BUCKETS = 32
REL_MAX_DIST = 2048
LRU_WIDTH = D_MODEL
LRU_BLOCKS = 4
LRU_BLOCK = LRU_WIDTH // LRU_BLOCKS
REC_CONV_K = 4
LRU_C = 8.0
D_FF = 2816
PLE_DIM = 256
EPS = 1e-6

kernel_name = "hybrid_conv_dilattn_rglru_macaron"


def rms_norm(x, gain):
    xf = x.astype(jnp.float32)
    y = xf * lax.rsqrt(jnp.mean(xf * xf, axis=-1, keepdims=True) + EPS)
    return (y * gain.astype(jnp.float32)).astype(x.dtype)


def swiglu(h, w_gate, w_up, w_down):
    return (jax.nn.silu(h @ w_gate) * (h @ w_up)) @ w_down


def causal_depthwise_conv(x, w):
    k_taps = w.shape[0]
    s = x.shape[1]
    xp = jnp.pad(x, ((0, 0), (k_taps - 1, 0), (0, 0)))
    y = xp[:, 0:s] * w[0]
    for j in range(1, k_taps):
        y = y + xp[:, j:j + s] * w[j]
    return y


def rel_bucket(dist):
    max_exact = REL_BUCKETS // 2
    n = jnp.maximum(dist, 1).astype(jnp.float32)
    large = max_exact + (jnp.log(n / max_exact) / math.log(REL_MAX_DIST / max_exact)
                         * (REL_BUCKETS - max_exact)).astype(jnp.int32)
    large = jnp.minimum(large, REL_BUCKETS - 1)
    return jnp.where(dist < max_exact, dist, large)


def dilated_branch(q, k, v, rel_bias, window, dilation):
    b_, s_, h_, dh = q.shape
    d = dilation
    nw = window // d
    sub_len = s_ // d
    nb = -(-sub_len // nw)
    lp = nb * nw

    def strided(t, front):
        t = t.reshape(b_, sub_len, d, h_, dh)
        return jnp.pad(t, ((0, 0), (front, lp - sub_len), (0, 0), (0, 0), (0, 0)))

    q_b = strided(q, 0).reshape(b_, nb, nw, d, h_, dh)
    k_p = strided(k, nw).reshape(b_, nb + 1, nw, d, h_, dh)
    v_p = strided(v, nw).reshape(b_, nb + 1, nw, d, h_, dh)
    k_b = jnp.concatenate([k_p[:, :-1], k_p[:, 1:]], axis=2)
    v_b = jnp.concatenate([v_p[:, :-1], v_p[:, 1:]], axis=2)

    scores = jnp.einsum('bnqrhe,bnkrhe->bnrhqk', q_b, k_b).astype(jnp.float32) * (dh ** -0.5)
    qi = jnp.arange(nw)[:, None]
    kj = jnp.arange(2 * nw)[None, :]
    dist = qi + nw - kj
    key_pos = jnp.arange(nb)[:, None, None] * nw + kj[None] - nw
    valid = (dist >= 0)[None] & (dist <= nw)[None] & (key_pos >= 0)
    bucket = rel_bucket(jnp.clip(dist, 0, nw) * d)
    bias = jnp.transpose(rel_bias[bucket].astype(jnp.float32), (2, 0, 1))
    logits = jnp.where(valid[None, :, None, None], scores + bias, -jnp.inf)
    lse = jax.nn.logsumexp(logits, axis=-1)
    probs = jnp.exp(logits - lse[..., None])
    out = jnp.einsum('bnrhqk,bnkrhe->bnqrhe', probs.astype(v.dtype), v_b)
    out = out.reshape(b_, lp, d, h_, dh)[:, :sub_len].reshape(b_, s_, h_, dh)
    lse = jnp.transpose(lse, (0, 1, 4, 2, 3)).reshape(b_, lp, d, h_)[:, :sub_len].reshape(b_, s_, h_)
    return out, lse


def hybrid_mixer(h, w_in, conv_w, q_gain, k_gain, rel_bias, w_out):
    b_, s_, _ = h.shape
    z = h @ w_in
    cuts = np.cumsum([CONV_WIDTH, CONV_WIDTH, CONV_WIDTH, ATTN_WIDTH, ATTN_WIDTH])
    g_b, g_c, c_x, q, k, v = jnp.split(z, cuts, axis=-1)
    y_conv = g_b * causal_depthwise_conv(g_c * c_x, conv_w)
    qh = rms_norm(q.reshape(b_, s_, ATTN_HEADS, HEAD_DIM), q_gain)
    kh = rms_norm(k.reshape(b_, s_, ATTN_HEADS, HEAD_DIM), k_gain)
    vh = v.reshape(b_, s_, ATTN_HEADS, HEAD_DIM)
    outs = []
    lses = []
    for window, dil in DILATED_PAIRS:
        o, l = dilated_branch(qh, kh, vh, rel_bias, window, dil)
        outs.append(o)
        lses.append(l)
    wts = jax.nn.softmax(jnp.stack(lses), axis=0)
    y_attn = jnp.sum(wts[..., None] * jnp.stack(outs).astype(jnp.float32), axis=0)
    y_attn = y_attn.astype(h.dtype).reshape(b_, s_, ATTN_WIDTH)
    return jnp.concatenate([y_conv, y_attn], axis=-1) @ w_out


def rg_lru(xb, wa, ba, wx, bx, lam):
    b_, s_, _ = xb.shape
    xf = xb.astype(jnp.float32)
    xr = xf.reshape(b_, s_, LRU_BLOCKS, LRU_BLOCK)
    gate_a = jnp.einsum('bsgi,gij->bsgj', xr, wa.astype(jnp.float32)).reshape(b_, s_, LRU_WIDTH) + ba.astype(jnp.float32)
    gate_x = jnp.einsum('bsgi,gij->bsgj', xr, wx.astype(jnp.float32)).reshape(b_, s_, LRU_WIDTH) + bx.astype(jnp.float32)
    log_a = -LRU_C * jax.nn.sigmoid(gate_a) * jax.nn.softplus(-lam.astype(jnp.float32))
    a = jnp.exp(log_a)
    u = jnp.sqrt(-jnp.expm1(2.0 * log_a)) * (jax.nn.sigmoid(gate_x) * xf)

    def combine(left, right):
        a1, b1 = left
        a2, b2 = right
        return a1 * a2, a2 * b1 + b2

    _, hs = lax.associative_scan(combine, (a, u), axis=1)
    return hs.astype(xb.dtype)


def recurrent_mixer(h, w_in, conv_w, conv_b, wa, ba, wx, bx, lam, w_out):
    z = h @ w_in
    xb, yb = jnp.split(z, 2, axis=-1)
    xb = causal_depthwise_conv(xb, conv_w) + conv_b
    return (rg_lru(xb, wa, ba, wx, bx, lam) * jax.nn.gelu(yb)) @ w_out


def setup_inputs(seed: int = 0) -> dict:
    key = jax.random.key(seed)
    ks = iter(jax.random.split(key, 40))
    f32 = jnp.float32

    def dense(shape, fan_in):
        return jax.random.normal(next(ks), shape, f32) * (fan_in ** -0.5)

    def gain(shape):
        return 1.0 + 0.02 * jax.random.normal(next(ks), shape, f32)

    def small(shape, scale=0.02):
        return scale * jax.random.normal(next(ks), shape, f32)

    u = jax.random.uniform(next(ks), (N_ODD, LRU_WIDTH), f32, 0.9, 0.999)
    s_base = u ** (1.0 / LRU_C)
    lru_lambda = jnp.log(s_base) - jnp.log1p(-s_base)

    return {
        "x": jax.random.normal(next(ks), (BATCH, SEQ, D_MODEL), f32),
        "p": jax.random.normal(next(ks), (DEPTH, BATCH, SEQ, PLE_DIM), f32),
        "rel_bias": small((REL_BUCKETS, ATTN_HEADS), 0.1),
        "ffn1_norm": gain((DEPTH, D_MODEL)),
        "ffn1_w_gate": dense((DEPTH, D_MODEL, D_FF), D_MODEL),
        "ffn1_w_up": dense((DEPTH, D_MODEL, D_FF), D_MODEL),
        "ffn1_w_down": dense((DEPTH, D_FF, D_MODEL), D_FF),
        "mix_norm": gain((DEPTH, D_MODEL)),
        "hyb_w_in": dense((N_EVEN, D_MODEL, MIX_IN), D_MODEL),
        "hyb_conv_w": dense((N_EVEN, SHORT_CONV_K, CONV_WIDTH), SHORT_CONV_K),
        "hyb_q_gain": gain((N_EVEN, HEAD_DIM)),
        "hyb_k_gain": gain((N_EVEN, HEAD_DIM)),
        "hyb_w_out": dense((N_EVEN, MIX_OUT, D_MODEL), MIX_OUT),
        "rec_w_in": dense((N_ODD, D_MODEL, 2 * LRU_WIDTH), D_MODEL),
        "rec_conv_w": dense((N_ODD, REC_CONV_K, LRU_WIDTH), REC_CONV_K),
        "rec_conv_b": small((N_ODD, LRU_WIDTH)),
        "lru_wa": dense((N_ODD, LRU_BLOCKS, LRU_BLOCK, LRU_BLOCK), LRU_BLOCK),
        "lru_ba": small((N_ODD, LRU_WIDTH)),
        "lru_wx": dense((N_ODD, LRU_BLOCKS, LRU_BLOCK, LRU_BLOCK), LRU_BLOCK),
        "lru_bx": small((N_ODD, LRU_WIDTH)),
        "lru_lambda": lru_lambda,
        "rec_w_out": dense((N_ODD, LRU_WIDTH, D_MODEL), LRU_WIDTH),
        "ffn2_norm": gain((DEPTH, D_MODEL)),
        "ffn2_w_gate": dense((DEPTH, D_MODEL, D_FF), D_MODEL),
        "ffn2_w_up": dense((DEPTH, D_MODEL, D_FF), D_MODEL),
        "ffn2_w_down": dense((DEPTH, D_FF, D_MODEL), D_FF),
        "ple_norm": gain((DEPTH, D_MODEL)),
        "ple_w_gate": dense((DEPTH, D_MODEL, D_MODEL), D_MODEL),
        "ple_w_proj": dense((DEPTH, PLE_DIM, D_MODEL), PLE_DIM),
    }


def reference(x, p, rel_bias, ffn1_norm, ffn1_w_gate, ffn1_w_up, ffn1_w_down, mix_norm,
              hyb_w_in, hyb_conv_w, hyb_q_gain, hyb_k_gain, hyb_w_out,
              rec_w_in, rec_conv_w, rec_conv_b, lru_wa, lru_ba, lru_wx, lru_bx, lru_lambda, rec_w_out,
              ffn2_norm, ffn2_w_gate, ffn2_w_up, ffn2_w_down, ple_norm, ple_w_gate, ple_w_proj):
    h = x
    for i in range(DEPTH):
        h = h + 0.5 * swiglu(rms_norm(h, ffn1_norm[i]), ffn1_w_gate[i], ffn1_w_up[i], ffn1_w_down[i])
        hn = rms_norm(h, mix_norm[i])
        if i % 2 == 0:
            e = i // 2
            h = h + hybrid_mixer(hn, hyb_w_in[e], hyb_conv_w[e], hyb_q_gain[e], hyb_k_gain[e],
                                 rel_bias, hyb_w_out[e])
        else:
            o = i // 2
            h = h + recurrent_mixer(hn, rec_w_in[o], rec_conv_w[o], rec_conv_b[o], lru_wa[o], lru_ba[o],
                                    lru_wx[o], lru_bx[o], lru_lambda[o], rec_w_out[o])
        h = h + 0.5 * swiglu(rms_norm(h, ffn2_norm[i]), ffn2_w_gate[i], ffn2_w_up[i], ffn2_w_down[i])
        gate = jax.nn.sigmoid(rms_norm(h, ple_norm[i]) @ ple_w_gate[i])
        h = h + gate * (p[i] @ ple_w_proj[i])
    return h
```

```python
import math
import numpy as np
from contextlib import ExitStack
import concourse.bass as bass
import concourse.mybir as mybir
from concourse.bass_utils import run_bass_kernel_spmd

F32 = mybir.dt.float32
BF16 = mybir.dt.bfloat16
AF = mybir.ActivationFunctionType
ALU = mybir.AluOpType

D = 1024
DFF = 2816
NFC = 22
T = 512
SEQ = 8192
EPS = 1e-6
ETW = 2944
NSLOT = 6
SLOTW = 3072
CH = 2048
ENGS = ("pe", "act", "dve", "pool", "sp")


class Sch:
    def __init__(self):
        self.streams = {e: [] for e in ENGS}
        self.semstreams = {}
        self.state = {}
        self.waited = {e: {} for e in ENGS}

    def op(self, eng, fn, reads=(), writes=(), dma=None):
        semkey = ("dma:" + dma) if dma else eng
        deps = {}

        def add(tok):
            if tok is None:
                return
            k, i = tok
            if deps.get(k, -1) < i:
                deps[k] = i

        raw_same = -1
        for k in reads:
            st = self.state.get(k)
            if st and st[0] is not None:
                add(st[0])
                if st[0][0] == eng and st[0][1] > raw_same:
                    raw_same = st[0][1]
        for k in writes:
            st = self.state.get(k)
            if st:
                add(st[0])
                for r in st[1]:
                    add(r)
        waits = []
        for k, i in deps.items():
            if k == eng and eng == "pe":
                continue
            if self.waited[eng].get(k, -1) >= i:
                continue
            self.waited[eng][k] = i
            waits.append((k, i))
        ss = self.semstreams.setdefault(semkey, [])
        idx = len(ss)
        rec = {"waits": waits, "fn": fn, "semkey": semkey, "idx": idx, "needs_inc": bool(dma), "dma": bool(dma)}
        ss.append(rec)
        self.streams[eng].append(rec)
        for (k, i) in waits:
            self.semstreams[k][i]["needs_inc"] = True
        tok = (semkey, idx)
        for k in reads:
            self.state.setdefault(k, [None, []])[1].append(tok)
        for k in writes:
            self.state[k] = [tok, []]
        return tok

    def final_wait(self, eng, toks):
        waits = []
        for (k, i) in toks:
            self.semstreams[k][i]["needs_inc"] = True
            waits.append((k, i))
        self.streams[eng].append({"waits": waits, "fn": None, "semkey": None, "idx": None,
                                  "needs_inc": False, "dma": False})

    def emit(self, nc, stack):
        sems = {}
        for k, ss in self.semstreams.items():
            sems[k] = stack.enter_context(nc.semaphore("s_" + k.replace(":", "_").replace(".", "_")))
            v = 0
            amt = 16 if k.startswith("dma:") else 1
            for rec in ss:
                if rec["needs_inc"]:
                    v += amt
                rec["val"] = v
        block = stack.enter_context(nc.Block())
        nw = {e: 0 for e in ENGS}

        def run(engname):
            def body(e):
                for rec in self.streams[engname]:
                    for (k, i) in rec["waits"]:
                        e.wait_ge(sems[k], self.semstreams[k][i]["val"])
                        nw[engname] += 1
                    if rec["fn"] is None:
                        continue
                    inst = rec["fn"](e)
                    if rec["needs_inc"]:
                        inst.then_inc(sems[rec["semkey"]], 16 if rec["dma"] else 1)
            return body

        block.tensor(run("pe"))
        block.scalar(run("act"))
        block.vector(run("dve"))
        block.gpsimd(run("pool"))
        block.sync(run("sp"))
        return {e: (len(self.streams[e]), nw[e]) for e in ENGS}


HYB_IN_OC = [16, 17, 18, 19, 12, 13, 14, 15, 4, 8, 5, 9, 6, 10, 7, 11, 0, 1, 2, 3]
REC_IN_OC = [x for g in range(4) for x in (2 * g, 2 * g + 1, 8 + 2 * g, 8 + 2 * g + 1)]


def section_table():
    secs = []
    for l in range(2):
        for w in (1, 2):
            secs.append((f"gu{l}{w}", NFC * 2048))
            secs.append((f"dn{l}{w}", 8 * DFF))
    secs += [("hyb_in", 20 * 1024), ("hyb_v", 4096), ("hyb_out", 8192), ("rec_in", 16 * 1024),
             ("lru", 4096), ("rec_out", 8192), ("pg0", 8192), ("pg1", 8192), ("pp0", 2048), ("pp1", 2048)]
    tab = {}
    off = 0
    for n, c in secs:
        tab[n] = (off, c)
        off += c
    tab["etab"] = (off, 8 * ETW)
    return tab, off, off + 8 * ETW


SEC, TOTW, TOTALL = section_table()


def tile_groups():
    g = []

    def sec(name, a, n):
        return (SEC[name][0] + a, n)

    def ffn(l, w):
        for fc in range(NFC):
            g.append((f"gu{l}{w}.{fc}",) + sec(f"gu{l}{w}", fc * 2048, 2048))
        for dc in range(8):
            g.append((f"dn{l}{w}.{dc}",) + sec(f"dn{l}{w}", dc * DFF, DFF))

    def ple(l):
        for j in range(4):
            g.append((f"pg{l}.{j}",) + sec(f"pg{l}", j * 2048, 2048))
        g.append((f"pp{l}",) + sec(f"pp{l}", 0, 2048))

    ffn(0, 1)
    for j in range(2):
        g.append((f"k.{j}",) + sec("hyb_in", j * 2048, 2048))
    for j in range(2):
        g.append((f"v.{j}",) + sec("hyb_v", j * 2048, 2048))
    for j in range(2):
        g.append((f"q.{j}",) + sec("hyb_in", 4096 + j * 2048, 2048))
    for h in range(8):
        g.append((f"e.{h}",) + sec("etab", h * ETW, ETW))
    for c in range(4):
        g.append((f"gccx.{c}",) + sec("hyb_in", 8192 + c * 2048, 2048))
    for j in range(2):
        g.append((f"gb.{j}",) + sec("hyb_in", 16384 + j * 2048, 2048))
    for j in range(4):
        g.append((f"ho.{j}",) + sec("hyb_out", j * 2048, 2048))
    ffn(0, 2)
    ple(0)
    ffn(1, 1)
    for b in range(4):
        g.append((f"rx.{b}",) + sec("rec_in", b * 4096, 2048))
        g.append((f"ry.{b}",) + sec("rec_in", b * 4096 + 2048, 2048))
        g.append((f"lru.{b}",) + sec("lru", b * 1024, 1024))
    for j in range(4):
        g.append((f"ro.{j}",) + sec("rec_out", j * 2048, 2048))
    ffn(1, 2)
    ple(1)
    return g


def oc_layout(W, nk, nm, order=None):
    A = W.reshape(nk, 128, nm, 128).transpose(1, 2, 0, 3)
    if order is not None:
        A = A[:, order]
    return np.ascontiguousarray(A).reshape(128, -1)


def pack_weights(inp):
    W = np.zeros((128, TOTW), np.float32)

    def put(name, arr):
        o, n = SEC[name]
        assert arr.shape == (128, n), (name, arr.shape, n)
        W[:, o:o + n] = arr

    for l in range(2):
        for w in (1, 2):
            Wg = inp[f"ffn{w}_w_gate"][l]
            Wu = inp[f"ffn{w}_w_up"][l]
            Wd = inp[f"ffn{w}_w_down"][l]
            G = Wg.reshape(8, 128, NFC, 128).transpose(1, 2, 0, 3)
            U = Wu.reshape(8, 128, NFC, 128).transpose(1, 2, 0, 3)
            put(f"gu{l}{w}", np.stack([G, U], axis=2).reshape(128, -1))
            put(f"dn{l}{w}", Wd.reshape(NFC, 128, 8, 128).transpose(1, 2, 0, 3).reshape(128, -1))
    hw = inp["hyb_w_in"][0]
    put("hyb_in", oc_layout(hw, 8, 24, HYB_IN_OC))
    put("hyb_v", hw[:, 2560:3072].reshape(8, 128, 512).transpose(1, 0, 2).reshape(128, -1))
    put("hyb_out", oc_layout(inp["hyb_w_out"][0], 8, 8))
    put("rec_in", oc_layout(inp["rec_w_in"][0], 8, 16, REC_IN_OC))
    wa = inp["lru_wa"][0].reshape(4, 2, 128, 2, 128).transpose(2, 0, 3, 1, 4)
    wx = inp["lru_wx"][0].reshape(4, 2, 128, 2, 128).transpose(2, 0, 3, 1, 4)
    put("lru", np.stack([wa, wx], axis=2).reshape(128, -1))
    put("rec_out", oc_layout(inp["rec_w_out"][0], 8, 8))
    for l in range(2):
        put(f"pg{l}", oc_layout(inp["ple_w_gate"][l], 8, 8))
        put(f"pp{l}", oc_layout(inp["ple_w_proj"][l], 2, 8))
    return W


def rel_bucket_np(dist):
    n = np.maximum(dist, 1).astype(np.float32)
    large = 16 + (np.log(n / np.float32(16)) / np.float32(math.log(2048 / 16)) * np.float32(16)).astype(np.int32)
    large = np.minimum(large, 31)
    return np.where(dist < 16, dist, large)


def attn_tables(rel_bias):
    p = np.arange(128)[:, None]
    x = np.arange(ETW)[None, :]
    dist = x - 384 - p
    valid = (dist >= 0) & (dist <= 2048)
    dc = np.clip(dist, 0, 2048)
    mult = ((dc <= 128).astype(np.float32) + ((dc % 4 == 0) & (dc <= 512)).astype(np.float32)
            + ((dc % 16 == 0) & (dc <= 2048)).astype(np.float32)) * valid
    bucket = rel_bucket_np(dc)
    G = np.stack([rel_bias[bucket, h] for h in range(8)], axis=1)
    return np.ascontiguousarray(G.astype(np.float32)), np.ascontiguousarray(mult.astype(np.float32))


def small_consts(inp):
    cols = []
    names = {}

    def add(name, arr):
        names[name] = (sum(c.shape[1] for c in cols), arr.shape[1])
        cols.append(np.ascontiguousarray(arr, dtype=np.float32))

    fm = lambda v: v.reshape(-1, 128).T
    for i, nm in enumerate(["ffn1_norm", "mix_norm", "ffn2_norm", "ple_norm"]):
        for l in range(2):
            add(f"g.{nm}.{l}", fm(inp[nm][l]))
    add("qg", np.tile(inp["hyb_q_gain"][0], 2)[:, None])
    add("kg", np.tile(inp["hyb_k_gain"][0], 2)[:, None])
    add("hcw", inp["hyb_conv_w"][0].reshape(3, 4, 128).transpose(2, 1, 0).reshape(128, 12))
    add("rcw", inp["rec_conv_w"][0].reshape(4, 8, 128).transpose(2, 1, 0).reshape(128, 32))
    add("rcb", fm(inp["rec_conv_b"][0]))
    add("ba", fm(inp["lru_ba"][0]))
    add("bx", fm(inp["lru_bx"][0]))
    add("lam", fm(inp["lru_lambda"][0]))
    k_ = np.arange(128)
    sw = np.zeros((128, 128), np.float32)
    sw[k_, (k_ + 64) % 128] = 1.0
    add("swp", sw)
    return np.concatenate(cols, axis=1), names


def build_program(seq=SEQ, taps=None, ncnames=None):
    taps = taps or {}
    NT = seq // T
    nc = bass.Bass("TRN2", target_bir_lowering=False)
    xT = nc.dram_tensor("xT", [D, seq], F32, kind="ExternalInput").ap()
    pT = nc.dram_tensor("pT", [2, 256, seq], F32, kind="ExternalInput").ap()
    wsrc = nc.dram_tensor("wsrc", [128, TOTW], F32, kind="ExternalInput").ap()
    gtab = nc.dram_tensor("gtab", [128, 8, ETW], F32, kind="ExternalInput").ap()
    mtab = nc.dram_tensor("mtab", [128, ETW], F32, kind="ExternalInput").ap()
    NCON = ncnames["_n"]
    cons_d = nc.dram_tensor("cons", [128, NCON], F32, kind="ExternalInput").ap()
    yT = nc.dram_tensor("yT", [D, seq], F32, kind="ExternalOutput").ap()
    wbf = nc.dram_tensor("wbf", [128, TOTALL], BF16).ap()
    xv = xT.rearrange("(c p) t -> p c t", p=128)
    yv = yT.rearrange("(c p) t -> p c t", p=128)
    pv = pT.rearrange("l (c p) t -> l p c t", p=128)
    tap_out = {}
    S = Sch()
    final_toks = []

    with ExitStack() as st:
        st.enter_context(nc.allow_low_precision("bf16 matmul operands, fp32 accumulate"))

        def sb(name, shape, dt):
            return st.enter_context(nc.sbuf_tensor(name, shape, dt))

        def ps(name):
            return st.enter_context(nc.psum_tensor(name, [128, T], F32))

        hbuf = [sb(f"h{i}", [128, 8, T], F32) for i in range(2)]
        hn = sb("hn", [128, 8, T], BF16)
        wring = sb("wring", [128, NSLOT, SLOTW], BF16)
        kring = sb("kring", [128, 4, 5 * T], BF16)
        vring = sb("vring", [128, 20, 768], BF16)
        arena = sb("arena", [128, NFC * T], BF16)
        ymix = sb("ymix", [128, 8, T], BF16)
        FS = [sb(f"fs{i}", [128, T], F32) for i in range(10)]
        GL = [sb(f"gl{i}", [128, T], F32) for i in range(4)]
        BS = [sb(f"bs{i}", [128, T], BF16) for i in range(4)]
        pbuf = sb("pbuf", [128, 2, T], F32)
        pbb = sb("pbb", [128, 2, T], BF16)
        cons = sb("cons_sb", [128, NCON], F32)
        ones = sb("ones", [128, 128], BF16)
        bd64 = sb("bd64", [128, 128], BF16)
        uhalo = sb("uhalo", [128, 4, 2], F32)
        xhalo = sb("xhalo", [128, 8, 3], F32)
        carry = sb("carry", [128, 8], F32)
        clam = sb("clam", [128, 8], F32)
        hcon = sb("hcon", [128, 24], F32)
        dummy = sb("dmy2", [128, 2], F32)
        PS = {"A": [ps("psA0"), ps("psA1")], "B": [ps("psB0"), ps("psB1")], "C": [ps("psC0"), ps("psC1")],
              "N": [ps("psN")], "M": [ps("psM")]}
        psctr = {k: 0 for k in PS}
        ctr = {"fs": 0, "bs": 0}

        def bank(k):
            i = psctr[k] % len(PS[k])
            psctr[k] += 1
            return PS[k][i], f"ps{k}{i}"

        def fs():
            i = ctr["fs"] % len(FS)
            ctr["fs"] += 1
            return FS[i], f"fs{i}"

        def bs():
            i = ctr["bs"] % len(BS)
            ctr["bs"] += 1
            return BS[i], f"bs{i}"

        def C(name):
            o, n = ncnames[name]
            return cons[:, o:o + n]

        def apages(off_bytes, nbytes):
            return [f"ar{pg}" for pg in range(off_bytes // 1024, (off_bytes + nbytes - 1) // 1024 + 1)]

        act_v = arena[:, :].rearrange("p (c t) -> p c t", t=T)

        def act_k(fc):
            return apages(fc * 1024, 1024)

        qn_v = arena[:, 0:8 * T].rearrange("p (c e t) -> p c e t", e=2, t=T)

        def qn_k(c):
            return apages(c * 2048, 2048)

        UB0 = 8192
        ub_v = arena[:, UB0 // 2: UB0 // 2 + 4 * 516 * 2].bitcast(F32).rearrange("p (c t) -> p c t", t=516)

        def ub_k(c):
            return apages(UB0 + c * 516 * 4, 516 * 4)

        XB0 = 0
        xb_v = arena[:, 0: 2 * 516 * 2].bitcast(F32).rearrange("p (c t) -> p c t", t=516)

        def xb_k(j):
            return apages(XB0 + j * 516 * 4, 516 * 4)

        XB2O = (0, 17408)
        xb2_v = [arena[:, o // 2: o // 2 + 2 * 516 * 2].bitcast(F32).rearrange("p (c t) -> p c t", t=516) for o in XB2O]

        def xb2_k(i, j):
            return apages(XB2O[i] + j * 516 * 4, 516 * 4)

        XCO = (9216, 13312)
        xc2_v = [arena[:, o // 2: o // 2 + 2 * T * 2].bitcast(F32).rearrange("p (c t) -> p c t", t=T) for o in XCO]

        def xc2_k(i, j):
            return apages(XCO[i] + j * 2048, 2048)

        XCBO = (4608, 6656)
        xcb2_v = [arena[:, o // 2: o // 2 + 2 * T].rearrange("p (c t) -> p c t", t=T) for o in XCBO]

        def xcb2_k(i, j):
            return apages(XCBO[i] + j * 1024, 1024)

        def tap(name, ap, keys, shape, dt=F32):
            dten = nc.dram_tensor("tap_" + name, list(shape), dt, kind="ExternalOutput").ap()
            tap_out[name] = (list(shape), dt)
            tok = S.op("pool", lambda e: e.dma_start(out=dten, in_=ap), reads=keys, writes=["tap_" + name], dma="tap_" + name)
            final_toks.append(tok)

        groups = tile_groups()
        NG = len(groups)
        wst = {"issued": 0, "next": 0}

        def wbf_keys(off, n):
            return [f"wbf{c}" for c in range(off // CH, (off + n - 1) // CH + 1)]

        def issue_load(n):
            name, off, ncols = groups[n % NG]
            slot = n % NSLOT
            S.op("sp", lambda e: e.dma_start(out=wring[:, slot, 0:ncols], in_=wbf[:, off:off + ncols]),
                 reads=wbf_keys(off, ncols), writes=[f"ws{slot}"], dma=f"ws{slot}")

        def acquire(expect, hold=0):
            n = wst["next"]
            wst["next"] += 1
            name = groups[n % NG][0]
            assert name == expect, (name, expect)
            total = NG * NT
            while wst["issued"] < min(total, n + NSLOT - hold):
                issue_load(wst["issued"])
                wst["issued"] += 1
            slot = n % NSLOT
            return wring[:, slot, :], f"ws{slot}"

        S.op("sp", lambda e: e.dma_start(out=cons[:, :], in_=cons_d), writes=["cons"], dma="cons")
        mt_sb = kring[:, :, :].rearrange("p c t -> p (c t)")[:, 0:2 * ETW].bitcast(F32)
        MTK = [f"kr{s_}.{c_}" for s_ in range(5) for c_ in range(4)]
        S.op("sp", lambda e: e.dma_start(out=mt_sb[:, :], in_=mtab), writes=MTK, dma="mt")
        S.op("dve", lambda e: e.memset(ones[:, :], 1.0), writes=["ones"])
        S.op("dve", lambda e: e.memset(dummy[:, :], 1.0), writes=["dummy"])
        S.op("dve", lambda e: e.memset(bd64[:, :], 0.0), writes=["bd64"])
        S.op("dve", lambda e: e.memset(bd64[0:64, 0:64], 1.0), writes=["bd64"])
        S.op("dve", lambda e: e.memset(bd64[64:128, 64:128], 1.0), writes=["bd64"])
        S.op("pool", lambda e: e.memset(vring[:, :, :], 1.0), writes=["vr%d" % b for b in range(20)])
        S.op("pool", lambda e: e.memset(uhalo[:, :, :], 0.0), writes=["uhalo"])
        S.op("pool", lambda e: e.memset(xhalo[:, :, :], 0.0), writes=[f"xhalo.{c}" for c in range(8)])
        S.op("pool", lambda e: e.memset(carry[:, :], 0.0), writes=[f"carry.{c}" for c in range(8)])
        S.op("act", lambda e: e.activation(clam[:, :], C("lam"), AF.Exp, scale=-1.0), reads=["cons"], writes=["clam"])
        S.op("act", lambda e: e.activation(clam[:, :], clam[:, :], AF.Ln, bias=1.0), reads=["clam"], writes=["clam"])
        S.op("dve", lambda e: e.tensor_scalar(clam[:, :], clam[:, :], -8.0, None, op0=ALU.mult), reads=["clam"], writes=["clam"])
        S.op("dve", lambda e: e.tensor_scalar(hcon[:, 0:8], C("ba"), 0.5, None, op0=ALU.mult), reads=["cons"], writes=["hcon"])
        S.op("dve", lambda e: e.tensor_scalar(hcon[:, 8:16], C("bx"), 0.5, None, op0=ALU.mult), reads=["cons"], writes=["hcon"])
        S.op("dve", lambda e: e.tensor_scalar(hcon[:, 16:24], clam[:, :], 0.5, None, op0=ALU.mult), reads=["clam"], writes=["hcon"])

        hflat = [hbuf[i][:, :, :].rearrange("p c t -> p (c t)") for i in range(2)]
        CB = 4 * CH
        nbig = (TOTW + CB - 1) // CB
        for k in range(nbig):
            c0 = k * CB
            n = min(CB, TOTW - c0)
            S.op("pool", lambda e, c0=c0, n=n: e.dma_start(out=wbf[:, c0:c0 + n], in_=wsrc[:, c0:c0 + n]),
                 writes=wbf_keys(c0, n), dma=f"cvt{k}")
        eoff = SEC["etab"][0]
        for h in range(8):
            j = h % 2
            gin = hflat[1][:, 0:ETW]
            gk = [f"h1.{c}" for c in range(6)]
            eo = ymix[:, :, :].rearrange("p c t -> p (c t)")[:, 0:ETW]
            ek = [f"ymix.{c}" for c in range(6)]
            S.op("act", lambda e, gin=gin, h=h: e.dma_start(out=gin, in_=gtab[:, h, :]), writes=gk, dma=f"ein{j}")
            S.op("act", lambda e, gin=gin: e.activation(gin, gin, AF.Exp), reads=gk, writes=gk)
            S.op("dve", lambda e, gin=gin, eo=eo: e.tensor_tensor(eo, gin, mt_sb[:, :], op=ALU.mult), reads=gk + MTK, writes=ek)
            o = eoff + h * ETW
            S.op("act", lambda e, eo=eo, o=o: e.dma_start(out=wbf[:, o:o + ETW], in_=eo), reads=ek,
                 writes=wbf_keys(o, ETW), dma=f"eout{j}")

        def hk(hb, c):
            return f"h{hb}.{c}"

        def hn_k(c):
            return f"hn.{c}"

        NS = {"pn": None, "pnk": None, "cnt": 0, "pend": None}

        def nsq_begin():
            pn, pnk = bank("N")
            NS.update(pn=pn, pnk=pnk, cnt=0, pend=None)
            S.op("act", lambda e: e.activation(dummy[:, 0:1], dummy[:, 1:2], AF.Ln), reads=["dummy"], writes=["dummy2"])

        def nsq_flush():
            if NS["pend"] is not None:
                sq, sqk = NS["pend"]
                i = NS["cnt"]
                pn, pnk = NS["pn"], NS["pnk"]
                S.op("pe", lambda e, sq=sq, i=i, pn=pn: e.matmul(pn[:, :], ones[:, :], sq[:, :], start=(i == 0), stop=(i == 7)),
                     reads=[sqk, "ones"], writes=[pnk])
                NS["cnt"] += 1
                NS["pend"] = None

        def nsq_chunk(hb, c):
            nsq_flush()
            h = hbuf[hb]
            sq, sqk = bs()
            S.op("act", lambda e, sq=sq, c=c: e.activation(sq[:, :], h[:, c, :], AF.Square), reads=[hk(hb, c)], writes=[sqk])
            NS["pend"] = (sq, sqk)

        def rmsnorm(hb, gname):
            h = hbuf[hb]
            nsq_flush()
            assert NS["cnt"] == 8, NS["cnt"]
            pn, pnk = NS["pn"], NS["pnk"]
            rs, rsk = fs()
            S.op("act", lambda e: e.activation(rs[:, :], pn[:, :], AF.Ln, scale=1.0 / D, bias=EPS), reads=[pnk], writes=[rsk])
            S.op("act", lambda e: e.activation(rs[:, :], rs[:, :], AF.Exp, scale=-0.5), reads=[rsk], writes=[rsk])
            g = C(gname)
            for c in range(8):
                S.op("dve", lambda e, c=c: e.scalar_tensor_tensor(hn[:, c, :], h[:, c, :], g[:, c:c + 1], rs[:, :], op0=ALU.mult, op1=ALU.mult),
                     reads=[hk(hb, c), rsk, "cons"], writes=[hn_k(c)])

        def ffn(hb, l, w, gname):
            h = hbuf[hb]
            rmsnorm(hb, gname)
            for fc in range(NFC):
                wv, wk = acquire(f"gu{l}{w}.{fc}")
                gu = wv[:, 0:2048].rearrange("p (a k m) -> p a k m", a=2, k=8)
                pg, pgk = bank("A")
                pu, puk = bank("B")
                for kc in range(8):
                    S.op("pe", lambda e, kc=kc, gu=gu, pg=pg: e.matmul(pg[:, :], gu[:, 0, kc, :], hn[:, kc, :], start=(kc == 0), stop=(kc == 7)),
                         reads=[wk, hn_k(kc)], writes=[pgk])
                for kc in range(8):
                    S.op("pe", lambda e, kc=kc, gu=gu, pu=pu: e.matmul(pu[:, :], gu[:, 1, kc, :], hn[:, kc, :], start=(kc == 0), stop=(kc == 7)),
                         reads=[wk, hn_k(kc)], writes=[puk])
                sg, sgk = fs()
                S.op("act", lambda e, sg=sg, pg=pg: e.activation(sg[:, :], pg[:, :], AF.Silu), reads=[pgk], writes=[sgk])
                S.op("dve", lambda e, sg=sg, pu=pu, fc=fc: e.tensor_tensor(act_v[:, fc, :], sg[:, :], pu[:, :], op=ALU.mult),
                     reads=[sgk, puk], writes=act_k(fc))
            nsq_begin()
            for dc in range(8):
                wv, wk = acquire(f"dn{l}{w}.{dc}")
                dn = wv[:, 0:DFF].rearrange("p (f m) -> p f m", f=NFC)
                pc, pck = bank("C")
                for fc in range(NFC):
                    S.op("pe", lambda e, fc=fc, dn=dn, pc=pc: e.matmul(pc[:, :], dn[:, fc, :], act_v[:, fc, :], start=(fc == 0), stop=(fc == NFC - 1)),
                         reads=[wk] + act_k(fc), writes=[pck])
                S.op("dve", lambda e, dc=dc, pc=pc: e.scalar_tensor_tensor(h[:, dc, :], pc[:, :], 0.5, h[:, dc, :], op0=ALU.mult, op1=ALU.add),
                     reads=[pck, hk(hb, dc)], writes=[hk(hb, dc)])
                nsq_chunk(hb, dc)

        def ple(hb, l, ti, nsq=True):
            h = hbuf[hb]
            t0 = ti * T
            S.op("act", lambda e: e.dma_start(out=pbuf[:, :, :], in_=pv[l, :, :, t0:t0 + T]), writes=["pbuf"], dma="pbuf")
            S.op("pool", lambda e: e.tensor_copy(pbb[:, :, :], pbuf[:, :, :]), reads=["pbuf"], writes=["pbb"])
            rmsnorm(hb, f"g.ple_norm.{l}")
            pending = []
            for oc in range(8):
                if oc % 2 == 0:
                    gv, gk = acquire(f"pg{l}.{oc // 2}")
                    gw = gv[:, 0:2048].rearrange("p (o k m) -> p o k m", o=2, k=8)
                pa, pak = bank("A")
                for kc in range(8):
                    S.op("pe", lambda e, kc=kc, gw=gw, pa=pa, oc=oc: e.matmul(pa[:, :], gw[:, oc % 2, kc, :], hn[:, kc, :], start=(kc == 0), stop=(kc == 7)),
                         reads=[gk, hn_k(kc)], writes=[pak])
                sg, sgk = fs()
                S.op("act", lambda e, sg=sg, pa=pa: e.activation(sg[:, :], pa[:, :], AF.Sigmoid), reads=[pak], writes=[sgk])
                pending.append((sg, sgk))
            pw, pwk = acquire(f"pp{l}")
            pw4 = pw[:, 0:2048].rearrange("p (o k m) -> p o k m", o=8, k=2)
            if nsq:
                nsq_begin()
            for oc in range(8):
                pb_, pbk = bank("A" if oc % 4 < 2 else "B")
                for kc in range(2):
                    S.op("pe", lambda e, kc=kc, oc=oc, pb_=pb_: e.matmul(pb_[:, :], pw4[:, oc, kc, :], pbb[:, kc, :], start=(kc == 0), stop=(kc == 1)),
                         reads=[pwk, "pbb"], writes=[pbk])
                sg, sgk = pending[oc]
                S.op("dve", lambda e, sg=sg, pb_=pb_: e.tensor_tensor(sg[:, :], sg[:, :], pb_[:, :], op=ALU.mult), reads=[sgk, pbk], writes=[sgk])
                S.op("pool", lambda e, sg=sg, oc=oc: e.tensor_tensor(h[:, oc, :], h[:, oc, :], sg[:, :], op=ALU.add),
                     reads=[sgk, hk(hb, oc)], writes=[hk(hb, oc)])
                if nsq:
                    nsq_chunk(hb, oc)

        def proj_out(hb, gname_prefix):
            h = hbuf[hb]
            nsq_begin()
            for oc in range(8):
                if oc % 2 == 0:
                    wv, wk = acquire(f"{gname_prefix}.{oc // 2}")
                    w4 = wv[:, 0:2048].rearrange("p (o k m) -> p o k m", o=2, k=8)
                pc, pck = bank("C")
                for kc in range(8):
                    S.op("pe", lambda e, kc=kc, w4=w4, pc=pc, oc=oc: e.matmul(pc[:, :], w4[:, oc % 2, kc, :], ymix[:, kc, :], start=(kc == 0), stop=(kc == 7)),
                         reads=[wk, f"ymix.{kc}"], writes=[pck])
                S.op("dve", lambda e, pc=pc, oc=oc: e.tensor_tensor(h[:, oc, :], h[:, oc, :], pc[:, :], op=ALU.add),
                     reads=[pck, hk(hb, oc)], writes=[hk(hb, oc)])
                nsq_chunk(hb, oc)

        def qk_chunk(w4, wk, o, gain, out_ap, out_keys):
            pa, pak = bank("A")
            for kc in range(8):
                S.op("pe", lambda e, kc=kc: e.matmul(pa[:, :], w4[:, o, kc, :], hn[:, kc, :], start=(kc == 0), stop=(kc == 7)),
                     reads=[wk, hn_k(kc)], writes=[pak])
            sq, sqk = bs()
            S.op("act", lambda e: e.activation(sq[:, :], pa[:, :], AF.Square), reads=[pak], writes=[sqk])
            pn, pnk = bank("N")
            S.op("pe", lambda e: e.matmul(pn[:, :], bd64[:, :], sq[:, :], start=True, stop=True), reads=[sqk, "bd64"], writes=[pnk])
            rs, rsk = fs()
            S.op("act", lambda e: e.activation(rs[:, :], pn[:, :], AF.Ln, scale=1.0 / 64, bias=EPS), reads=[pnk], writes=[rsk])
            S.op("act", lambda e: e.activation(rs[:, :], rs[:, :], AF.Exp, scale=-0.5), reads=[rsk], writes=[rsk])
            if isinstance(out_ap, tuple):
                for hh, oa in enumerate(out_ap):
                    pr_ = slice(hh * 64, hh * 64 + 64)
                    S.op("dve", lambda e, oa=oa, pr_=pr_: e.scalar_tensor_tensor(oa, pa[pr_, :], gain[pr_, 0:1], rs[pr_, :], op0=ALU.mult, op1=ALU.mult),
                         reads=[pak, rsk, "cons"], writes=out_keys)
            else:
                S.op("dve", lambda e: e.scalar_tensor_tensor(out_ap, pa[:, :], gain[:, 0:1], rs[:, :], op0=ALU.mult, op1=ALU.mult),
                     reads=[pak, rsk, "cons"], writes=out_keys)

        def mixer0(hb, ti):
            h = hbuf[hb]
            T0 = ti * T
            rmsnorm(hb, "g.mix_norm.0")
            kslot = ti % 5
            for c in range(4):
                if c % 2 == 0:
                    wv, wk = acquire(f"k.{c // 2}")
                    w4 = wv[:, 0:2048].rearrange("p (o k m) -> p o k m", o=2, k=8)
                qk_chunk(w4, wk, c % 2, C("kg"), kring[:, c, kslot * T:(kslot + 1) * T], [f"kr{kslot}.{c}"])
            wv0, wk0 = acquire("v.0")
            wv1, wk1 = acquire("v.1", hold=1)
            v0 = wv0[:, 0:2048].rearrange("p (k m) -> p k m", k=4)
            v1 = wv1[:, 0:2048].rearrange("p (k m) -> p k m", k=4)
            for tb in range(4):
                pb_, pbk = bank("B")
                for kc in range(8):
                    vv = v0 if kc < 4 else v1
                    vk = wk0 if kc < 4 else wk1
                    S.op("pe", lambda e, kc=kc, vv=vv, pb_=pb_, tb=tb: e.matmul(pb_[:, :], hn[:, kc, tb * 128:(tb + 1) * 128], vv[:, kc % 4, :],
                                                                             start=(kc == 0), stop=(kc == 7)),
                         reads=[vk, hn_k(kc)], writes=[pbk])
                blk = ((T0 // 128) + tb) % 20
                dst = vring[:, blk, :].rearrange("p (c x) -> p c x", x=192)
                src = pb_[:, :].rearrange("p (c e d) -> p c e d", e=2, d=64)
                S.op("act", lambda e, dst=dst, src=src: e.copy(dst[:, :, 0:64], src[:, :, 0, :]), reads=[pbk], writes=[f"vr{blk}"])
                S.op("dve", lambda e, dst=dst, src=src: e.tensor_copy(dst[:, :, 128:192], src[:, :, 1, :]), reads=[pbk], writes=[f"vr{blk}"])
            for c in range(4):
                if c % 2 == 0:
                    wv, wk = acquire(f"q.{c // 2}")
                    w4 = wv[:, 0:2048].rearrange("p (o k m) -> p o k m", o=2, k=8)
                S.op("pool", lambda e, c=c: e.memset(qn_v[64:128, c, 0, :], 0.0), writes=qn_k(c))
                S.op("pool", lambda e, c=c: e.memset(qn_v[0:64, c, 1, :], 0.0), writes=qn_k(c))
                qk_chunk(w4, wk, c % 2, C("qg"), (qn_v[0:64, c, 0, :], qn_v[64:128, c, 1, :]), qn_k(c))
            if "qn" in taps and taps["qn"] == ti:
                tap("qn", qn_v[:, :, :, :], [k_ for c_ in range(4) for k_ in qn_k(c_)], [128, 4, 2, T], BF16)
                tap("kn", kring[:, :, kslot * T:(kslot + 1) * T], [f"kr{kslot}.{c_}" for c_ in range(4)], [128, 4, T], BF16)
                b0_ = (T0 // 128) % 20
                tap("vr", vring[:, b0_:b0_ + 4, :], [f"vr{b0_ + i_}" for i_ in range(4)], [128, 4, 768], BF16)
            deltas = [0] + [d for d in range(-384, 2049, 128) if d != 0]
            blocks = []
            for dl in deltas:
                s0 = T0 - dl
                if s0 < 0:
                    continue
                q0 = max(0, -dl)
                q1 = min(T, 2176 - dl)
                blocks.append((dl, s0, q0, q1))
            nb = len(blocks)
            pair_out = {}
            for hd in range(8):
                c, half = hd // 2, hd % 2
                pr = slice(half * 64, half * 64 + 64)
                ev, ek = acquire(f"e.{hd}")
                po, pok = bank("C")
                vcol = c * 192 + half * 64

                def qk(bi, hd=hd, c=c, pr=pr):
                    dl, s0, q0, q1 = blocks[bi]
                    pa, pak = bank("A")
                    ks = (s0 // T) % 5
                    kc0 = ks * T + (s0 % T)
                    S.op("pe", lambda e: e.matmul(pa[:, q0:q1], kring[:, c, kc0:kc0 + 128], qn_v[:, c, hd % 2, q0:q1], start=True, stop=True),
                         reads=[f"kr{ks}.{c}"] + qn_k(c), writes=[pak])
                    return pa, pak

                cur = qk(0)
                for bi in range(nb):
                    dl, s0, q0, q1 = blocks[bi]
                    pa, pak = cur
                    if bi + 1 < nb:
                        cur = qk(bi + 1)
                    ex, exk = bs()
                    S.op("act", lambda e, ex=ex, pa=pa, q0=q0, q1=q1: e.activation(ex[:, q0:q1], pa[:, q0:q1], AF.Exp, scale=0.125),
                         reads=[pak], writes=[exk])
                    x0 = dl + 384 + q0
                    S.op("dve", lambda e, ex=ex, q0=q0, q1=q1, x0=x0, ev=ev: e.tensor_tensor(ex[:, q0:q1], ex[:, q0:q1], ev[:, x0:x0 + (q1 - q0)], op=ALU.mult),
                         reads=[exk, ek], writes=[exk])
                    blk = (s0 // 128) % 20
                    S.op("pe", lambda e, ex=ex, q0=q0, q1=q1, blk=blk, po=po, bi=bi, vcol=vcol: e.matmul(
                        po[:, q0:q1], vring[:, blk, vcol:vcol + 128], ex[:, q0:q1], start=(bi == 0), stop=(bi == nb - 1), skip_group_check=True),
                         reads=[exk, f"vr{blk}"], writes=[pok])
                pair_out[half] = (po, pok)
                if half == 1:
                    (pe_, pek), (po_, pok_) = pair_out[0], pair_out[1]
                    num, numk = fs()
                    den, denk = fs()
                    S.op("act", lambda e, num=num, pe_=pe_: e.copy(num[0:64, :], pe_[0:64, :]), reads=[pek], writes=[numk])
                    S.op("dve", lambda e, den=den, pe_=pe_: e.tensor_copy(den[64:128, :], pe_[64:128, :]), reads=[pek], writes=[denk])
                    S.op("act", lambda e, num=num, po_=po_: e.copy(num[64:128, :], po_[64:128, :]), reads=[pok_], writes=[numk])
                    S.op("dve", lambda e, den=den, po_=po_: e.tensor_copy(den[0:64, :], po_[0:64, :]), reads=[pok_], writes=[denk])
                    pm, pmk = bank("M")
                    S.op("pe", lambda e, den=den, pm=pm: e.matmul(pm[:, :], C("swp"), den[:, :], start=True, stop=True), reads=[denk, "cons"], writes=[pmk])
                    S.op("act", lambda e, den=den, pm=pm: e.activation(den[:, :], pm[:, :], AF.Ln), reads=[pmk], writes=[denk])
                    S.op("act", lambda e, den=den: e.activation(den[:, :], den[:, :], AF.Exp, scale=-1.0), reads=[denk], writes=[denk])
                    S.op("dve", lambda e, den=den, num=num, c=c: e.tensor_tensor(ymix[:, 4 + c, :], num[:, :], den[:, :], op=ALU.mult),
                         reads=[numk, denk], writes=[f"ymix.{4 + c}"])
            cw = C("hcw")
            gcs = []
            for c in range(4):
                wv, wk = acquire(f"gccx.{c}")
                w4 = wv[:, 0:2048].rearrange("p (o k m) -> p o k m", o=2, k=8)
                pa, pak = bank("A")
                pb_, pbk = bank("B")
                for kc in range(8):
                    S.op("pe", lambda e, kc=kc, w4=w4, pa=pa: e.matmul(pa[:, :], w4[:, 0, kc, :], hn[:, kc, :], start=(kc == 0), stop=(kc == 7)),
                         reads=[wk, hn_k(kc)], writes=[pak])
                for kc in range(8):
                    S.op("pe", lambda e, kc=kc, w4=w4, pb_=pb_: e.matmul(pb_[:, :], w4[:, 1, kc, :], hn[:, kc, :], start=(kc == 0), stop=(kc == 7)),
                         reads=[wk, hn_k(kc)], writes=[pbk])
                t1, t1k = fs()
                S.op("act", lambda e, t1=t1, pa=pa: e.copy(t1[:, :], pa[:, :]), reads=[pak], writes=[t1k])
                S.op("pool", lambda e, c=c: e.tensor_copy(ub_v[:, c, 0:2], uhalo[:, c, :]), reads=["uhalo"], writes=ub_k(c))
                S.op("dve", lambda e, t1=t1, pb_=pb_, c=c: e.tensor_tensor(ub_v[:, c, 2:2 + T], t1[:, :], pb_[:, :], op=ALU.mult),
                     reads=[t1k, pbk], writes=ub_k(c))
                S.op("pool", lambda e, c=c: e.tensor_copy(uhalo[:, c, :], ub_v[:, c, T:T + 2]), reads=ub_k(c), writes=["uhalo"])
                S.op("dve", lambda e, t1=t1, c=c: e.tensor_scalar(t1[:, :], ub_v[:, c, 2:2 + T], cw[:, c * 3 + 2:c * 3 + 3], None, op0=ALU.mult),
                     reads=ub_k(c) + ["cons"], writes=[t1k])
                S.op("dve", lambda e, t1=t1, c=c: e.scalar_tensor_tensor(t1[:, :], ub_v[:, c, 1:1 + T], cw[:, c * 3 + 1:c * 3 + 2], t1[:, :], op0=ALU.mult, op1=ALU.add),
                     reads=ub_k(c) + [t1k, "cons"], writes=[t1k])
                S.op("dve", lambda e, t1=t1, c=c: e.scalar_tensor_tensor(t1[:, :], ub_v[:, c, 0:T], cw[:, c * 3:c * 3 + 1], t1[:, :], op0=ALU.mult, op1=ALU.add),
                     reads=ub_k(c) + [t1k, "cons"], writes=[t1k])
                gcs.append((t1, t1k))
            for c in range(4):
                if c % 2 == 0:
                    wv, wk = acquire(f"gb.{c // 2}")
                    w4 = wv[:, 0:2048].rearrange("p (o k m) -> p o k m", o=2, k=8)
                pa, pak = bank("A")
                for kc in range(8):
                    S.op("pe", lambda e, kc=kc, w4=w4, pa=pa, c=c: e.matmul(pa[:, :], w4[:, c % 2, kc, :], hn[:, kc, :], start=(kc == 0), stop=(kc == 7)),
                         reads=[wk, hn_k(kc)], writes=[pak])
                t1, t1k = gcs[c]
                S.op("dve", lambda e, t1=t1, pa=pa, c=c: e.tensor_tensor(ymix[:, c, :], t1[:, :], pa[:, :], op=ALU.mult),
                     reads=[t1k, pak], writes=[f"ymix.{c}"])
            if "ymix0" in taps and taps["ymix0"] == ti:
                tap("ymix0", ymix[:, :, :], [f"ymix.{c}" for c in range(8)], [128, 8, T], BF16)
            proj_out(hb, "ho")

        def mixer1(hb, ti):
            h = hbuf[hb]
            rmsnorm(hb, "g.mix_norm.1")
            cw = C("rcw")
            cb = C("rcb")

            def proj(b):
                nh = 0 if b == 0 else 1
                xv_, xk = acquire(f"rx.{b}", hold=nh)
                yv_, yk = acquire(f"ry.{b}", hold=nh + 1)
                lv, lk = acquire(f"lru.{b}", hold=nh + 2)
                wx4 = xv_[:, 0:2048].rearrange("p (o k m) -> p o k m", o=2, k=8)
                wy4 = yv_[:, 0:2048].rearrange("p (o k m) -> p o k m", o=2, k=8)
                l5 = lv[:, 0:1024].rearrange("p (a j k m) -> p a j k m", a=2, j=2, k=2)
                xbv = xb2_v[b % 2]
                for j in range(2):
                    c = 2 * b + j
                    pa, pak = bank("A")
                    for kc in range(8):
                        S.op("pe", lambda e, kc=kc, pa=pa, j=j, wx4=wx4: e.matmul(pa[:, :], wx4[:, j, kc, :], hn[:, kc, :], start=(kc == 0), stop=(kc == 7)),
                             reads=[xk, hn_k(kc)], writes=[pak])
                    S.op("pool", lambda e, j=j, c=c, xbv=xbv: e.tensor_copy(xbv[:, j, 0:3], xhalo[:, c, :]), reads=[f"xhalo.{c}"], writes=xb2_k(b % 2, j))
                    S.op("act", lambda e, j=j, pa=pa, xbv=xbv: e.copy(xbv[:, j, 3:3 + T], pa[:, :]), reads=[pak], writes=xb2_k(b % 2, j))
                    S.op("pool", lambda e, j=j, c=c, xbv=xbv: e.tensor_copy(xhalo[:, c, :], xbv[:, j, T:T + 3]), reads=xb2_k(b % 2, j), writes=[f"xhalo.{c}"])
                gls = []
                for j in range(2):
                    pb_, pbk = bank("B")
                    for kc in range(8):
                        S.op("pe", lambda e, kc=kc, pb_=pb_, j=j, wy4=wy4: e.matmul(pb_[:, :], wy4[:, j, kc, :], hn[:, kc, :], start=(kc == 0), stop=(kc == 7)),
                             reads=[yk, hn_k(kc)], writes=[pbk])
                    gl, glk = GL[(b % 2) * 2 + j], f"gl{(b % 2) * 2 + j}"
                    S.op("act", lambda e, gl=gl, pb_=pb_: e.activation(gl[:, :], pb_[:, :], AF.Gelu_apprx_tanh), reads=[pbk], writes=[glk])
                    gls.append((gl, glk))
                return {"l5": l5, "lk": lk, "gls": gls, "xbv": xbv}

            def chain_a(b, stt):
                xbv = stt["xbv"]
                xc_v, xcb_v = xc2_v[b % 2], xcb2_v[b % 2]
                for j in range(2):
                    c = 2 * b + j
                    S.op("dve", lambda e, j=j, c=c: e.tensor_scalar(xc_v[:, j, :], xbv[:, j, 3:3 + T], cw[:, c * 4 + 3:c * 4 + 4], cb[:, c:c + 1], op0=ALU.mult, op1=ALU.add),
                         reads=xb2_k(b % 2, j) + ["cons"], writes=xc2_k(b % 2, j))
                    for tp in range(3):
                        S.op("dve", lambda e, j=j, c=c, tp=tp: e.scalar_tensor_tensor(xc_v[:, j, :], xbv[:, j, tp:tp + T], cw[:, c * 4 + tp:c * 4 + tp + 1], xc_v[:, j, :],
                                                                              op0=ALU.mult, op1=ALU.add),
                             reads=xb2_k(b % 2, j) + xc2_k(b % 2, j) + ["cons"], writes=xc2_k(b % 2, j))
                    S.op("act", lambda e, j=j: e.copy(xcb_v[:, j, :], xc_v[:, j, :]), reads=xc2_k(b % 2, j), writes=xcb2_k(b % 2, j))

            def chain_b(b, stt):
                l5, lk, gls = stt["l5"], stt["lk"], stt["gls"]
                xc_v, xcb_v = xc2_v[b % 2], xcb2_v[b % 2]
                pgs = []
                for j in range(2):
                    pga, pgak = (bank("M") if j == 0 else bank("C"))
                    pgx, pgxk = (bank("N") if j == 0 else bank("C"))
                    for kc in range(2):
                        S.op("pe", lambda e, kc=kc, j=j, pga=pga: e.matmul(pga[:, :], l5[:, 0, j, kc, :], xcb_v[:, kc, :], start=(kc == 0), stop=(kc == 1)),
                             reads=[lk] + xcb2_k(b % 2, kc), writes=[pgak])
                    for kc in range(2):
                        S.op("pe", lambda e, kc=kc, j=j, pgx=pgx: e.matmul(pgx[:, :], l5[:, 1, j, kc, :], xcb_v[:, kc, :], start=(kc == 0), stop=(kc == 1)),
                             reads=[lk] + xcb2_k(b % 2, kc), writes=[pgxk])
                    pgs.append((pga, pgak, pgx, pgxk))
                ch = []
                for j in range(2):
                    c = 2 * b + j
                    pga, pgak, pgx, pgxk = pgs[j]
                    ra, rak = fs()
                    ri, rik = fs()
                    S.op("act", lambda e, ra=ra, pga=pga, c=c: e.activation(ra[:, :], pga[:, :], AF.Tanh, scale=0.5, bias=hcon[:, c:c + 1]), reads=[pgak, "hcon"], writes=[rak])
                    S.op("act", lambda e, ri=ri, pgx=pgx, c=c: e.activation(ri[:, :], pgx[:, :], AF.Tanh, scale=0.5, bias=hcon[:, 8 + c:9 + c]), reads=[pgxk, "hcon"], writes=[rik])
                    S.op("act", lambda e, ra=ra, c=c: e.activation(ra[:, :], ra[:, :], AF.Exp, scale=hcon[:, 16 + c:17 + c], bias=hcon[:, 16 + c:17 + c]), reads=[rak, "hcon"], writes=[rak])
                    ch.append([ra, rak, ri, rik])
                for j in range(2):
                    ra, rak, ri, rik = ch[j]
                    mm_, mmk = fs()
                    S.op("dve", lambda e, mm_=mm_, ra=ra: e.tensor_tensor(mm_[:, :], ra[:, :], ra[:, :], op=ALU.mult), reads=[rak], writes=[mmk])
                    S.op("dve", lambda e, ri=ri, j=j: e.scalar_tensor_tensor(ri[:, :], ri[:, :], 1.0, xc_v[:, j, :], op0=ALU.add, op1=ALU.mult), reads=[rik] + xc2_k(b % 2, j), writes=[rik])
                    ch[j] += [mm_, mmk]
                for j in range(2):
                    ra, rak, ri, rik, mm_, mmk = ch[j]
                    S.op("act", lambda e, mm_=mm_: e.activation(mm_[:, :], mm_[:, :], AF.Sqrt, scale=-0.25, bias=0.25), reads=[mmk], writes=[mmk])
                for j in range(2):
                    c = 2 * b + j
                    ra, rak, ri, rik, mm_, mmk = ch[j]
                    gl, glk = gls[j]
                    S.op("dve", lambda e, ri=ri, mm_=mm_: e.tensor_tensor(ri[:, :], ri[:, :], mm_[:, :], op=ALU.mult), reads=[rik, mmk], writes=[rik])
                    S.op("dve", lambda e, mm_=mm_, ra=ra, ri=ri, c=c: e.tensor_tensor_scan(mm_[:, :], ra[:, :], ri[:, :], carry[:, c:c + 1], op0=ALU.mult, op1=ALU.add),
                         reads=[rak, rik, f"carry.{c}"], writes=[mmk])
                    S.op("act", lambda e, mm_=mm_, c=c: e.copy(carry[:, c:c + 1], mm_[:, T - 1:T]), reads=[mmk], writes=[f"carry.{c}"])
                    S.op("dve", lambda e, gl=gl, mm_=mm_, c=c: e.tensor_tensor(ymix[:, c, :], mm_[:, :], gl[:, :], op=ALU.mult),
                         reads=[mmk, glk], writes=[f"ymix.{c}"])

            sts = {0: proj(0)}
            chain_a(0, sts[0])
            for b in range(4):
                if b < 3:
                    sts[b + 1] = proj(b + 1)
                    chain_a(b + 1, sts[b + 1])
                chain_b(b, sts[b])
            if "ymix1" in taps and taps["ymix1"] == ti:
                tap("ymix1", ymix[:, :, :], [f"ymix.{c}" for c in range(8)], [128, 8, T], BF16)
            proj_out(hb, "ro")

        def load_x(ti):
            hb = ti % 2
            S.op("act", lambda e: e.dma_start(out=hbuf[hb][:, :, :], in_=xv[:, :, ti * T:(ti + 1) * T]),
                 writes=[hk(hb, c) for c in range(8)], dma=f"x{hb}")

        def htap(name, hb, ti):
            if name in taps and taps[name] == ti:
                tap(name, hbuf[hb][:, :, :], [hk(hb, c) for c in range(8)], [128, 8, T], F32)

        load_x(0)
        for ti in range(NT):
            hb = ti % 2
            nsq_begin()
            for c_ in range(8):
                nsq_chunk(hb, c_)
            ffn(hb, 0, 1, "g.ffn1_norm.0")
            if ti + 1 < NT:
                load_x(ti + 1)
            htap("h_ffn1_0", hb, ti)
            mixer0(hb, ti)
            htap("h_mix_0", hb, ti)
            ffn(hb, 0, 2, "g.ffn2_norm.0")
            ple(hb, 0, ti)
            htap("h_l0", hb, ti)
            ffn(hb, 1, 1, "g.ffn1_norm.1")
            mixer1(hb, ti)
            htap("h_mix_1", hb, ti)
            ffn(hb, 1, 2, "g.ffn2_norm.1")
            ple(hb, 1, ti, nsq=False)
            tok = S.op("act", lambda e, hb=hb, ti=ti: e.dma_start(out=yv[:, :, ti * T:(ti + 1) * T], in_=hbuf[hb][:, :, :]),
                       reads=[hk(hb, c) for c in range(8)], writes=[f"yout{hb}"], dma=f"y{hb}")
            final_toks.append(tok)
        S.final_wait("act", final_toks)
        stats = S.emit(nc, st)
    return nc, stats, tap_out


def prep_common(inp):
    W = pack_weights(inp)
    G, M = attn_tables(np.asarray(inp["rel_bias"], np.float32))
    cons, names = small_consts(inp)
    names = dict(names)
    names["_n"] = cons.shape[1]
    return W, G, M, np.ascontiguousarray(cons), names


def kernel(**inputs):
    inp = {k: np.asarray(v) for k, v in inputs.items()}
    W, G, M, cons, names = prep_common(inp)
    x = inp["x"]
    p = inp["p"]
    B = x.shape[0]
    nc, stats, _ = build_program(SEQ, None, names)
    in_maps = []
    for b in range(B):
        in_maps.append({
            "xT": np.ascontiguousarray(x[b].T),
            "pT": np.ascontiguousarray(p[:, b].transpose(0, 2, 1)),
            "wsrc": W, "gtab": G, "mtab": M, "cons": cons,
        })
    res = run_bass_kernel_spmd(nc, in_maps, core_ids=list(range(B)))
    out = np.stack([np.ascontiguousarray(res.results[b]["yT"].T) for b in range(B)], axis=0)
    return out.astype(np.float32)
```

```python
import math
import numpy as np
from contextlib import ExitStack
import concourse.bass as bass
import concourse.mybir as mybir
from concourse.bass_utils import run_bass_kernel_spmd

F32 = mybir.dt.float32
BF16 = mybir.dt.bfloat16
AF = mybir.ActivationFunctionType
ALU = mybir.AluOpType

D = 1024
DFF = 2816
NFC = 22
T = 512
SEQ = 8192
EPS = 1e-6
ETW = 2944
NSLOT = 6
SLOTW = 3072
CH = 2048
ENGS = ("pe", "act", "dve", "pool", "sp")


class Sch:
    def __init__(self):
        self.streams = {e: [] for e in ENGS}
        self.semstreams = {}
        self.state = {}
        self.waited = {e: {} for e in ENGS}

    def op(self, eng, fn, reads=(), writes=(), dma=None):
        semkey = ("dma:" + dma) if dma else eng
        deps = {}

        def add(tok):
            if tok is None:
                return
            k, i = tok
            if deps.get(k, -1) < i:
                deps[k] = i

        raw_same = -1
        for k in reads:
            st = self.state.get(k)
            if st and st[0] is not None:
                add(st[0])
                if st[0][0] == eng and st[0][1] > raw_same:
                    raw_same = st[0][1]
        for k in writes:
            st = self.state.get(k)
            if st:
                add(st[0])
                for r in st[1]:
                    add(r)
        waits = []
        for k, i in deps.items():
            if k == eng and eng == "pe":
                continue
            if self.waited[eng].get(k, -1) >= i:
                continue
            self.waited[eng][k] = i
            waits.append((k, i))
        ss = self.semstreams.setdefault(semkey, [])
        idx = len(ss)
        rec = {"waits": waits, "fn": fn, "semkey": semkey, "idx": idx, "needs_inc": bool(dma), "dma": bool(dma)}
        ss.append(rec)
        self.streams[eng].append(rec)
        for (k, i) in waits:
            self.semstreams[k][i]["needs_inc"] = True
        tok = (semkey, idx)
        for k in reads:
            self.state.setdefault(k, [None, []])[1].append(tok)
        for k in writes:
            self.state[k] = [tok, []]
        return tok

    def final_wait(self, eng, toks):
        waits = []
        for (k, i) in toks:
            self.semstreams[k][i]["needs_inc"] = True
            waits.append((k, i))
        self.streams[eng].append({"waits": waits, "fn": None, "semkey": None, "idx": None,
                                  "needs_inc": False, "dma": False})

    def emit(self, nc, stack):
        sems = {}
        for k, ss in self.semstreams.items():
            sems[k] = stack.enter_context(nc.semaphore("s_" + k.replace(":", "_").replace(".", "_")))
            v = 0
            amt = 16 if k.startswith("dma:") else 1
            for rec in ss:
                if rec["needs_inc"]:
                    v += amt
                rec["val"] = v
        block = stack.enter_context(nc.Block())
        nw = {e: 0 for e in ENGS}

        def run(engname):
            def body(e):
                for rec in self.streams[engname]:
                    for (k, i) in rec["waits"]:
                        e.wait_ge(sems[k], self.semstreams[k][i]["val"])
                        nw[engname] += 1
                    if rec["fn"] is None:
                        continue
                    inst = rec["fn"](e)
                    if rec["needs_inc"]:
                        inst.then_inc(sems[rec["semkey"]], 16 if rec["dma"] else 1)
            return body

        block.tensor(run("pe"))
        block.scalar(run("act"))
        block.vector(run("dve"))
        block.gpsimd(run("pool"))
        block.sync(run("sp"))
        return {e: (len(self.streams[e]), nw[e]) for e in ENGS}


HYB_IN_OC = [16, 17, 18, 19, 12, 13, 14, 15, 4, 8, 5, 9, 6, 10, 7, 11, 0, 1, 2, 3]
REC_IN_OC = [x for g in range(4) for x in (2 * g, 2 * g + 1, 8 + 2 * g, 8 + 2 * g + 1)]


def section_table():
    secs = []
    for l in range(2):
        for w in (1, 2):
            secs.append((f"gu{l}{w}", NFC * 2048))
            secs.append((f"dn{l}{w}", 8 * DFF))
    secs += [("hyb_in", 20 * 1024), ("hyb_v", 4096), ("hyb_out", 8192), ("rec_in", 16 * 1024),
             ("lru", 4096), ("rec_out", 8192), ("pg0", 8192), ("pg1", 8192), ("pp0", 2048), ("pp1", 2048)]
    tab = {}
    off = 0
    for n, c in secs:
        tab[n] = (off, c)
        off += c
    tab["etab"] = (off, 8 * ETW)
    return tab, off, off + 8 * ETW


SEC, TOTW, TOTALL = section_table()


def tile_groups():
    g = []

    def sec(name, a, n):
        return (SEC[name][0] + a, n)

    def ffn(l, w):
        for fc in range(NFC):
            g.append((f"gu{l}{w}.{fc}",) + sec(f"gu{l}{w}", fc * 2048, 2048))
        for dc in range(8):
            g.append((f"dn{l}{w}.{dc}",) + sec(f"dn{l}{w}", dc * DFF, DFF))

    def ple(l):
        for j in range(4):
            g.append((f"pg{l}.{j}",) + sec(f"pg{l}", j * 2048, 2048))
        g.append((f"pp{l}",) + sec(f"pp{l}", 0, 2048))

    ffn(0, 1)
    for j in range(2):
        g.append((f"k.{j}",) + sec("hyb_in", j * 2048, 2048))
    for j in range(2):
        g.append((f"v.{j}",) + sec("hyb_v", j * 2048, 2048))
    for j in range(2):
        g.append((f"q.{j}",) + sec("hyb_in", 4096 + j * 2048, 2048))
    for h in range(8):
        g.append((f"e.{h}",) + sec("etab", h * ETW, ETW))
    for c in range(4):
        g.append((f"gccx.{c}",) + sec("hyb_in", 8192 + c * 2048, 2048))
    for j in range(2):
        g.append((f"gb.{j}",) + sec("hyb_in", 16384 + j * 2048, 2048))
    for j in range(4):
        g.append((f"ho.{j}",) + sec("hyb_out", j * 2048, 2048))
    ffn(0, 2)
    ple(0)
    ffn(1, 1)
    for b in range(4):
        g.append((f"rx.{b}",) + sec("rec_in", b * 4096, 2048))
        g.append((f"ry.{b}",) + sec("rec_in", b * 4096 + 2048, 2048))
        g.append((f"lru.{b}",) + sec("lru", b * 1024, 1024))
    for j in range(4):
        g.append((f"ro.{j}",) + sec("rec_out", j * 2048, 2048))
    ffn(1, 2)
    ple(1)
    return g


def oc_layout(W, nk, nm, order=None):
    A = W.reshape(nk, 128, nm, 128).transpose(1, 2, 0, 3)
    if order is not None:
        A = A[:, order]
    return np.ascontiguousarray(A).reshape(128, -1)


def pack_weights(inp):
    W = np.zeros((128, TOTW), np.float32)

    def put(name, arr):
        o, n = SEC[name]
        assert arr.shape == (128, n), (name, arr.shape, n)
        W[:, o:o + n] = arr

    for l in range(2):
        for w in (1, 2):
            Wg = inp[f"ffn{w}_w_gate"][l]
            Wu = inp[f"ffn{w}_w_up"][l]
            Wd = inp[f"ffn{w}_w_down"][l]
            G = Wg.reshape(8, 128, NFC, 128).transpose(1, 2, 0, 3)
            U = Wu.reshape(8, 128, NFC, 128).transpose(1, 2, 0, 3)
            put(f"gu{l}{w}", np.stack([G, U], axis=2).reshape(128, -1))
            put(f"dn{l}{w}", Wd.reshape(NFC, 128, 8, 128).transpose(1, 2, 0, 3).reshape(128, -1))
    hw = inp["hyb_w_in"][0]
    put("hyb_in", oc_layout(hw, 8, 24, HYB_IN_OC))
    put("hyb_v", hw[:, 2560:3072].reshape(8, 128, 512).transpose(1, 0, 2).reshape(128, -1))
    put("hyb_out", oc_layout(inp["hyb_w_out"][0], 8, 8))
    put("rec_in", oc_layout(inp["rec_w_in"][0], 8, 16, REC_IN_OC))
    wa = inp["lru_wa"][0].reshape(4, 2, 128, 2, 128).transpose(2, 0, 3, 1, 4)
    wx = inp["lru_wx"][0].reshape(4, 2, 128, 2, 128).transpose(2, 0, 3, 1, 4)
    put("lru", np.stack([wa, wx], axis=2).reshape(128, -1))
    put("rec_out", oc_layout(inp["rec_w_out"][0], 8, 8))
    for l in range(2):
        put(f"pg{l}", oc_layout(inp["ple_w_gate"][l], 8, 8))
        put(f"pp{l}", oc_layout(inp["ple_w_proj"][l], 2, 8))
    return W


def rel_bucket_np(dist):
    n = np.maximum(dist, 1).astype(np.float32)
    large = 16 + (np.log(n / np.float32(16)) / np.float32(math.log(2048 / 16)) * np.float32(16)).astype(np.int32)
    large = np.minimum(large, 31)
    return np.where(dist < 16, dist, large)


def attn_tables(rel_bias):
    p = np.arange(128)[:, None]
    x = np.arange(ETW)[None, :]
    dist = x - 384 - p
    valid = (dist >= 0) & (dist <= 2048)
    dc = np.clip(dist, 0, 2048)
    mult = ((dc <= 128).astype(np.float32) + ((dc % 4 == 0) & (dc <= 512)).astype(np.float32)
            + ((dc % 16 == 0) & (dc <= 2048)).astype(np.float32)) * valid
    bucket = rel_bucket_np(dc)
    G = np.stack([rel_bias[bucket, h] for h in range(8)], axis=1)
    return np.ascontiguousarray(G.astype(np.float32)), np.ascontiguousarray(mult.astype(np.float32))


def small_consts(inp):
    cols = []
    names = {}

    def add(name, arr):
        names[name] = (sum(c.shape[1] for c in cols), arr.shape[1])
        cols.append(np.ascontiguousarray(arr, dtype=np.float32))

    fm = lambda v: v.reshape(-1, 128).T
    for i, nm in enumerate(["ffn1_norm", "mix_norm", "ffn2_norm", "ple_norm"]):
        for l in range(2):
            add(f"g.{nm}.{l}", fm(inp[nm][l]))
    add("qg", np.tile(inp["hyb_q_gain"][0], 2)[:, None])
    add("kg", np.tile(inp["hyb_k_gain"][0], 2)[:, None])
    add("hcw", inp["hyb_conv_w"][0].reshape(3, 4, 128).transpose(2, 1, 0).reshape(128, 12))
    add("rcw", inp["rec_conv_w"][0].reshape(4, 8, 128).transpose(2, 1, 0).reshape(128, 32))
    add("rcb", fm(inp["rec_conv_b"][0]))
    add("ba", fm(inp["lru_ba"][0]))
    add("bx", fm(inp["lru_bx"][0]))
    add("lam", fm(inp["lru_lambda"][0]))
    k_ = np.arange(128)
    sw = np.zeros((128, 128), np.float32)
    sw[k_, (k_ + 64) % 128] = 1.0
    add("swp", sw)
    return np.concatenate(cols, axis=1), names


def build_program(seq=SEQ, taps=None, ncnames=None):
    taps = taps or {}
    NT = seq // T
    nc = bass.Bass("TRN2", target_bir_lowering=False)
    xT = nc.dram_tensor("xT", [D, seq], F32, kind="ExternalInput").ap()
    pT = nc.dram_tensor("pT", [2, 256, seq], F32, kind="ExternalInput").ap()
    wsrc = nc.dram_tensor("wsrc", [128, TOTW], F32, kind="ExternalInput").ap()
    gtab = nc.dram_tensor("gtab", [128, 8, ETW], F32, kind="ExternalInput").ap()
    mtab = nc.dram_tensor("mtab", [128, ETW], F32, kind="ExternalInput").ap()
    NCON = ncnames["_n"]
    cons_d = nc.dram_tensor("cons", [128, NCON], F32, kind="ExternalInput").ap()
    yT = nc.dram_tensor("yT", [D, seq], F32, kind="ExternalOutput").ap()
    wbf = nc.dram_tensor("wbf", [128, TOTALL], BF16).ap()
    xv = xT.rearrange("(c p) t -> p c t", p=128)
    yv = yT.rearrange("(c p) t -> p c t", p=128)
    pv = pT.rearrange("l (c p) t -> l p c t", p=128)
    tap_out = {}
    S = Sch()
    final_toks = []

    with ExitStack() as st:
        st.enter_context(nc.allow_low_precision("bf16 matmul operands, fp32 accumulate"))

        def sb(name, shape, dt):
            return st.enter_context(nc.sbuf_tensor(name, shape, dt))

        def ps(name):
            return st.enter_context(nc.psum_tensor(name, [128, T], F32))

        hbuf = [sb(f"h{i}", [128, 8, T], F32) for i in range(2)]
        hn = sb("hn", [128, 8, T], BF16)
        wring = sb("wring", [128, NSLOT, SLOTW], BF16)
        kring = sb("kring", [128, 4, 5 * T], BF16)
        vring = sb("vring", [128, 20, 768], BF16)
        arena = sb("arena", [128, NFC * T], BF16)
        ymix = sb("ymix", [128, 8, T], BF16)
        FS = [sb(f"fs{i}", [128, T], F32) for i in range(10)]
        GL = [sb(f"gl{i}", [128, T], F32) for i in range(4)]
        BS = [sb(f"bs{i}", [128, T], BF16) for i in range(4)]
        pbuf = sb("pbuf", [128, 2, T], F32)
        pbb = sb("pbb", [128, 2, T], BF16)
        cons = sb("cons_sb", [128, NCON], F32)
        ones = sb("ones", [128, 128], BF16)
        bd64 = sb("bd64", [128, 128], BF16)
        uhalo = sb("uhalo", [128, 4, 2], F32)
        xhalo = sb("xhalo", [128, 8, 3], F32)
        carry = sb("carry", [128, 8], F32)
        clam = sb("clam", [128, 8], F32)
        hcon = sb("hcon", [128, 24], F32)
        dummy = sb("dmy2", [128, 2], F32)
        PS = {"A": [ps("psA0"), ps("psA1")], "B": [ps("psB0"), ps("psB1")], "C": [ps("psC0"), ps("psC1")],
              "N": [ps("psN")], "M": [ps("psM")]}
        psctr = {k: 0 for k in PS}
        ctr = {"fs": 0, "bs": 0}

        def bank(k):
            i = psctr[k] % len(PS[k])
            psctr[k] += 1
            return PS[k][i], f"ps{k}{i}"

        def fs():
            i = ctr["fs"] % len(FS)
            ctr["fs"] += 1
            return FS[i], f"fs{i}"

        def bs():
            i = ctr["bs"] % len(BS)
            ctr["bs"] += 1
            return BS[i], f"bs{i}"

        def C(name):
            o, n = ncnames[name]
            return cons[:, o:o + n]

        def apages(off_bytes, nbytes):
            return [f"ar{pg}" for pg in range(off_bytes // 1024, (off_bytes + nbytes - 1) // 1024 + 1)]

        act_v = arena[:, :].rearrange("p (c t) -> p c t", t=T)

        def act_k(fc):
            return apages(fc * 1024, 1024)

        qn_v = arena[:, 0:8 * T].rearrange("p (c e t) -> p c e t", e=2, t=T)

        def qn_k(c):
            return apages(c * 2048, 2048)

        UB0 = 8192
        ub_v = arena[:, UB0 // 2: UB0 // 2 + 4 * 516 * 2].bitcast(F32).rearrange("p (c t) -> p c t", t=516)

        def ub_k(c):
            return apages(UB0 + c * 516 * 4, 516 * 4)

        XB0 = 0
        xb_v = arena[:, 0: 2 * 516 * 2].bitcast(F32).rearrange("p (c t) -> p c t", t=516)

        def xb_k(j):
            return apages(XB0 + j * 516 * 4, 516 * 4)

        XB2O = (0, 17408)
        xb2_v = [arena[:, o // 2: o // 2 + 2 * 516 * 2].bitcast(F32).rearrange("p (c t) -> p c t", t=516) for o in XB2O]

        def xb2_k(i, j):
            return apages(XB2O[i] + j * 516 * 4, 516 * 4)

        XCO = (9216, 13312)
        xc2_v = [arena[:, o // 2: o // 2 + 2 * T * 2].bitcast(F32).rearrange("p (c t) -> p c t", t=T) for o in XCO]

        def xc2_k(i, j):
            return apages(XCO[i] + j * 2048, 2048)

        XCBO = (4608, 6656)
        xcb2_v = [arena[:, o // 2: o // 2 + 2 * T].rearrange("p (c t) -> p c t", t=T) for o in XCBO]

        def xcb2_k(i, j):
            return apages(XCBO[i] + j * 1024, 1024)

        def tap(name, ap, keys, shape, dt=F32):
            dten = nc.dram_tensor("tap_" + name, list(shape), dt, kind="ExternalOutput").ap()
            tap_out[name] = (list(shape), dt)
            tok = S.op("pool", lambda e: e.dma_start(out=dten, in_=ap), reads=keys, writes=["tap_" + name], dma="tap_" + name)
            final_toks.append(tok)

        groups = tile_groups()
        NG = len(groups)
        wst = {"issued": 0, "next": 0}

        def wbf_keys(off, n):
            return [f"wbf{c}" for c in range(off // CH, (off + n - 1) // CH + 1)]

        def issue_load(n):
            name, off, ncols = groups[n % NG]
            slot = n % NSLOT
            S.op("sp", lambda e: e.dma_start(out=wring[:, slot, 0:ncols], in_=wbf[:, off:off + ncols]),
                 reads=wbf_keys(off, ncols), writes=[f"ws{slot}"], dma=f"ws{slot}")

        def acquire(expect, hold=0):
            n = wst["next"]
            wst["next"] += 1
            name = groups[n % NG][0]
            assert name == expect, (name, expect)
            total = NG * NT
            while wst["issued"] < min(total, n + NSLOT - hold):
                issue_load(wst["issued"])
                wst["issued"] += 1
            slot = n % NSLOT
            return wring[:, slot, :], f"ws{slot}"

        S.op("sp", lambda e: e.dma_start(out=cons[:, :], in_=cons_d), writes=["cons"], dma="cons")
        mt_sb = kring[:, :, :].rearrange("p c t -> p (c t)")[:, 0:2 * ETW].bitcast(F32)
        MTK = [f"kr{s_}.{c_}" for s_ in range(5) for c_ in range(4)]
        S.op("sp", lambda e: e.dma_start(out=mt_sb[:, :], in_=mtab), writes=MTK, dma="mt")
        S.op("dve", lambda e: e.memset(ones[:, :], 1.0), writes=["ones"])
        S.op("dve", lambda e: e.memset(dummy[:, :], 1.0), writes=["dummy"])
        S.op("dve", lambda e: e.memset(bd64[:, :], 0.0), writes=["bd64"])
        S.op("dve", lambda e: e.memset(bd64[0:64, 0:64], 1.0), writes=["bd64"])
        S.op("dve", lambda e: e.memset(bd64[64:128, 64:128], 1.0), writes=["bd64"])
        S.op("pool", lambda e: e.memset(vring[:, :, :], 1.0), writes=["vr%d" % b for b in range(20)])
        S.op("pool", lambda e: e.memset(uhalo[:, :, :], 0.0), writes=["uhalo"])
        S.op("pool", lambda e: e.memset(xhalo[:, :, :], 0.0), writes=[f"xhalo.{c}" for c in range(8)])
        S.op("pool", lambda e: e.memset(carry[:, :], 0.0), writes=[f"carry.{c}" for c in range(8)])
        S.op("act", lambda e: e.activation(clam[:, :], C("lam"), AF.Exp, scale=-1.0), reads=["cons"], writes=["clam"])
        S.op("act", lambda e: e.activation(clam[:, :], clam[:, :], AF.Ln, bias=1.0), reads=["clam"], writes=["clam"])
        S.op("dve", lambda e: e.tensor_scalar(clam[:, :], clam[:, :], -8.0, None, op0=ALU.mult), reads=["clam"], writes=["clam"])
        S.op("dve", lambda e: e.tensor_scalar(hcon[:, 0:8], C("ba"), 0.5, None, op0=ALU.mult), reads=["cons"], writes=["hcon"])
        S.op("dve", lambda e: e.tensor_scalar(hcon[:, 8:16], C("bx"), 0.5, None, op0=ALU.mult), reads=["cons"], writes=["hcon"])
        S.op("dve", lambda e: e.tensor_scalar(hcon[:, 16:24], clam[:, :], 0.5, None, op0=ALU.mult), reads=["clam"], writes=["hcon"])

        hflat = [hbuf[i][:, :, :].rearrange("p c t -> p (c t)") for i in range(2)]
        CB = 4 * CH
        nbig = (TOTW + CB - 1) // CB
        for k in range(nbig):
            c0 = k * CB
            n = min(CB, TOTW - c0)
            S.op("pool", lambda e, c0=c0, n=n: e.dma_start(out=wbf[:, c0:c0 + n], in_=wsrc[:, c0:c0 + n]),
                 writes=wbf_keys(c0, n), dma=f"cvt{k}")
        eoff = SEC["etab"][0]
        for h in range(8):
            j = h % 2
            gin = hflat[1][:, 0:ETW]
            gk = [f"h1.{c}" for c in range(6)]
            eo = ymix[:, :, :].rearrange("p c t -> p (c t)")[:, 0:ETW]
            ek = [f"ymix.{c}" for c in range(6)]
            S.op("act", lambda e, gin=gin, h=h: e.dma_start(out=gin, in_=gtab[:, h, :]), writes=gk, dma=f"ein{j}")
            S.op("act", lambda e, gin=gin: e.activation(gin, gin, AF.Exp), reads=gk, writes=gk)
            S.op("dve", lambda e, gin=gin, eo=eo: e.tensor_tensor(eo, gin, mt_sb[:, :], op=ALU.mult), reads=gk + MTK, writes=ek)
            o = eoff + h * ETW
            S.op("act", lambda e, eo=eo, o=o: e.dma_start(out=wbf[:, o:o + ETW], in_=eo), reads=ek,
                 writes=wbf_keys(o, ETW), dma=f"eout{j}")

        def hk(hb, c):
            return f"h{hb}.{c}"

        def hn_k(c):
            return f"hn.{c}"

        NS = {"pn": None, "pnk": None, "cnt": 0, "pend": None}

        def nsq_begin():
            pn, pnk = bank("N")
            NS.update(pn=pn, pnk=pnk, cnt=0, pend=None)
            S.op("act", lambda e: e.activation(dummy[:, 0:1], dummy[:, 1:2], AF.Ln), reads=["dummy"], writes=["dummy2"])

        def nsq_flush():
            if NS["pend"] is not None:
                sq, sqk = NS["pend"]
                i = NS["cnt"]
                pn, pnk = NS["pn"], NS["pnk"]
                S.op("pe", lambda e, sq=sq, i=i, pn=pn: e.matmul(pn[:, :], ones[:, :], sq[:, :], start=(i == 0), stop=(i == 7)),
                     reads=[sqk, "ones"], writes=[pnk])
                NS["cnt"] += 1
                NS["pend"] = None

        def nsq_chunk(hb, c):
            nsq_flush()
            h = hbuf[hb]
            sq, sqk = bs()
            S.op("act", lambda e, sq=sq, c=c: e.activation(sq[:, :], h[:, c, :], AF.Square), reads=[hk(hb, c)], writes=[sqk])
            NS["pend"] = (sq, sqk)

        def rmsnorm(hb, gname):
            h = hbuf[hb]
            nsq_flush()
            assert NS["cnt"] == 8, NS["cnt"]
            pn, pnk = NS["pn"], NS["pnk"]
            rs, rsk = fs()
            S.op("act", lambda e: e.activation(rs[:, :], pn[:, :], AF.Ln, scale=1.0 / D, bias=EPS), reads=[pnk], writes=[rsk])
            S.op("act", lambda e: e.activation(rs[:, :], rs[:, :], AF.Exp, scale=-0.5), reads=[rsk], writes=[rsk])
            g = C(gname)
            for c in range(8):
                S.op("dve", lambda e, c=c: e.scalar_tensor_tensor(hn[:, c, :], h[:, c, :], g[:, c:c + 1], rs[:, :], op0=ALU.mult, op1=ALU.mult),
                     reads=[hk(hb, c), rsk, "cons"], writes=[hn_k(c)])

        def ffn(hb, l, w, gname):
            h = hbuf[hb]
            rmsnorm(hb, gname)
            for fc in range(NFC):
                wv, wk = acquire(f"gu{l}{w}.{fc}")
                gu = wv[:, 0:2048].rearrange("p (a k m) -> p a k m", a=2, k=8)
                pg, pgk = bank("A")
                pu, puk = bank("B")
                for kc in range(8):
                    S.op("pe", lambda e, kc=kc, gu=gu, pg=pg: e.matmul(pg[:, :], gu[:, 0, kc, :], hn[:, kc, :], start=(kc == 0), stop=(kc == 7)),
                         reads=[wk, hn_k(kc)], writes=[pgk])
                for kc in range(8):
                    S.op("pe", lambda e, kc=kc, gu=gu, pu=pu: e.matmul(pu[:, :], gu[:, 1, kc, :], hn[:, kc, :], start=(kc == 0), stop=(kc == 7)),
                         reads=[wk, hn_k(kc)], writes=[puk])
                sg, sgk = fs()
                S.op("act", lambda e, sg=sg, pg=pg: e.activation(sg[:, :], pg[:, :], AF.Silu), reads=[pgk], writes=[sgk])
                S.op("dve", lambda e, sg=sg, pu=pu, fc=fc: e.tensor_tensor(act_v[:, fc, :], sg[:, :], pu[:, :], op=ALU.mult),
                     reads=[sgk, puk], writes=act_k(fc))
            nsq_begin()
            for dc in range(8):
                wv, wk = acquire(f"dn{l}{w}.{dc}")
                dn = wv[:, 0:DFF].rearrange("p (f m) -> p f m", f=NFC)
                pc, pck = bank("C")
                for fc in range(NFC):
                    S.op("pe", lambda e, fc=fc, dn=dn, pc=pc: e.matmul(pc[:, :], dn[:, fc, :], act_v[:, fc, :], start=(fc == 0), stop=(fc == NFC - 1)),
                         reads=[wk] + act_k(fc), writes=[pck])
                S.op("dve", lambda e, dc=dc, pc=pc: e.scalar_tensor_tensor(h[:, dc, :], pc[:, :], 0.5, h[:, dc, :], op0=ALU.mult, op1=ALU.add),
                     reads=[pck, hk(hb, dc)], writes=[hk(hb, dc)])
                nsq_chunk(hb, dc)

        def ple(hb, l, ti, nsq=True):
            h = hbuf[hb]
            t0 = ti * T
            S.op("act", lambda e: e.dma_start(out=pbuf[:, :, :], in_=pv[l, :, :, t0:t0 + T]), writes=["pbuf"], dma="pbuf")
            S.op("pool", lambda e: e.tensor_copy(pbb[:, :, :], pbuf[:, :, :]), reads=["pbuf"], writes=["pbb"])
            rmsnorm(hb, f"g.ple_norm.{l}")
            pending = []
            for oc in range(8):
                if oc % 2 == 0:
                    gv, gk = acquire(f"pg{l}.{oc // 2}")
                    gw = gv[:, 0:2048].rearrange("p (o k m) -> p o k m", o=2, k=8)
                pa, pak = bank("A")
                for kc in range(8):
                    S.op("pe", lambda e, kc=kc, gw=gw, pa=pa, oc=oc: e.matmul(pa[:, :], gw[:, oc % 2, kc, :], hn[:, kc, :], start=(kc == 0), stop=(kc == 7)),
                         reads=[gk, hn_k(kc)], writes=[pak])
                sg, sgk = fs()
                S.op("act", lambda e, sg=sg, pa=pa: e.activation(sg[:, :], pa[:, :], AF.Sigmoid), reads=[pak], writes=[sgk])
                pending.append((sg, sgk))
            pw, pwk = acquire(f"pp{l}")
            pw4 = pw[:, 0:2048].rearrange("p (o k m) -> p o k m", o=8, k=2)
            if nsq:
                nsq_begin()
            for oc in range(8):
                pb_, pbk = bank("A" if oc % 4 < 2 else "B")
                for kc in range(2):
                    S.op("pe", lambda e, kc=kc, oc=oc, pb_=pb_: e.matmul(pb_[:, :], pw4[:, oc, kc, :], pbb[:, kc, :], start=(kc == 0), stop=(kc == 1)),
                         reads=[pwk, "pbb"], writes=[pbk])
                sg, sgk = pending[oc]
                S.op("dve", lambda e, sg=sg, pb_=pb_: e.tensor_tensor(sg[:, :], sg[:, :], pb_[:, :], op=ALU.mult), reads=[sgk, pbk], writes=[sgk])
                S.op("pool", lambda e, sg=sg, oc=oc: e.tensor_tensor(h[:, oc, :], h[:, oc, :], sg[:, :], op=ALU.add),
                     reads=[sgk, hk(hb, oc)], writes=[hk(hb, oc)])
                if nsq:
                    nsq_chunk(hb, oc)

        def proj_out(hb, gname_prefix):
            h = hbuf[hb]
            nsq_begin()
            for oc in range(8):
                if oc % 2 == 0:
                    wv, wk = acquire(f"{gname_prefix}.{oc // 2}")
                    w4 = wv[:, 0:2048].rearrange("p (o k m) -> p o k m", o=2, k=8)
                pc, pck = bank("C")
                for kc in range(8):
                    S.op("pe", lambda e, kc=kc, w4=w4, pc=pc, oc=oc: e.matmul(pc[:, :], w4[:, oc % 2, kc, :], ymix[:, kc, :], start=(kc == 0), stop=(kc == 7)),
                         reads=[wk, f"ymix.{kc}"], writes=[pck])
                S.op("dve", lambda e, pc=pc, oc=oc: e.tensor_tensor(h[:, oc, :], h[:, oc, :], pc[:, :], op=ALU.add),
                     reads=[pck, hk(hb, oc)], writes=[hk(hb, oc)])
                nsq_chunk(hb, oc)

        def qk_chunk(w4, wk, o, gain, out_ap, out_keys):
            pa, pak = bank("A")
            for kc in range(8):
                S.op("pe", lambda e, kc=kc: e.matmul(pa[:, :], w4[:, o, kc, :], hn[:, kc, :], start=(kc == 0), stop=(kc == 7)),
                     reads=[wk, hn_k(kc)], writes=[pak])
            sq, sqk = bs()
            S.op("act", lambda e: e.activation(sq[:, :], pa[:, :], AF.Square), reads=[pak], writes=[sqk])
            pn, pnk = bank("N")
            S.op("pe", lambda e: e.matmul(pn[:, :], bd64[:, :], sq[:, :], start=True, stop=True), reads=[sqk, "bd64"], writes=[pnk])
            rs, rsk = fs()
            S.op("act", lambda e: e.activation(rs[:, :], pn[:, :], AF.Ln, scale=1.0 / 64, bias=EPS), reads=[pnk], writes=[rsk])
            S.op("act", lambda e: e.activation(rs[:, :], rs[:, :], AF.Exp, scale=-0.5), reads=[rsk], writes=[rsk])
            if isinstance(out_ap, tuple):
                for hh, oa in enumerate(out_ap):
                    pr_ = slice(hh * 64, hh * 64 + 64)
                    S.op("dve", lambda e, oa=oa, pr_=pr_: e.scalar_tensor_tensor(oa, pa[pr_, :], gain[pr_, 0:1], rs[pr_, :], op0=ALU.mult, op1=ALU.mult),
                         reads=[pak, rsk, "cons"], writes=out_keys)
            else:
                S.op("dve", lambda e: e.scalar_tensor_tensor(out_ap, pa[:, :], gain[:, 0:1], rs[:, :], op0=ALU.mult, op1=ALU.mult),
                     reads=[pak, rsk, "cons"], writes=out_keys)

        def mixer0(hb, ti):
            h = hbuf[hb]
            T0 = ti * T
            rmsnorm(hb, "g.mix_norm.0")
            kslot = ti % 5
            for c in range(4):
                if c % 2 == 0:
                    wv, wk = acquire(f"k.{c // 2}")
                    w4 = wv[:, 0:2048].rearrange("p (o k m) -> p o k m", o=2, k=8)
                qk_chunk(w4, wk, c % 2, C("kg"), kring[:, c, kslot * T:(kslot + 1) * T], [f"kr{kslot}.{c}"])
            wv0, wk0 = acquire("v.0")
            wv1, wk1 = acquire("v.1", hold=1)
            v0 = wv0[:, 0:2048].rearrange("p (k m) -> p k m", k=4)
            v1 = wv1[:, 0:2048].rearrange("p (k m) -> p k m", k=4)
            for tb in range(4):
                pb_, pbk = bank("B")
                for kc in range(8):
                    vv = v0 if kc < 4 else v1
                    vk = wk0 if kc < 4 else wk1
                    S.op("pe", lambda e, kc=kc, vv=vv, pb_=pb_, tb=tb: e.matmul(pb_[:, :], hn[:, kc, tb * 128:(tb + 1) * 128], vv[:, kc % 4, :],
                                                                             start=(kc == 0), stop=(kc == 7)),
                         reads=[vk, hn_k(kc)], writes=[pbk])
                blk = ((T0 // 128) + tb) % 20
                dst = vring[:, blk, :].rearrange("p (c x) -> p c x", x=192)
                src = pb_[:, :].rearrange("p (c e d) -> p c e d", e=2, d=64)
                S.op("act", lambda e, dst=dst, src=src: e.copy(dst[:, :, 0:64], src[:, :, 0, :]), reads=[pbk], writes=[f"vr{blk}"])
                S.op("dve", lambda e, dst=dst, src=src: e.tensor_copy(dst[:, :, 128:192], src[:, :, 1, :]), reads=[pbk], writes=[f"vr{blk}"])
            for c in range(4):
                if c % 2 == 0:
                    wv, wk = acquire(f"q.{c // 2}")
                    w4 = wv[:, 0:2048].rearrange("p (o k m) -> p o k m", o=2, k=8)
                S.op("pool", lambda e, c=c: e.memset(qn_v[64:128, c, 0, :], 0.0), writes=qn_k(c))
                S.op("pool", lambda e, c=c: e.memset(qn_v[0:64, c, 1, :], 0.0), writes=qn_k(c))
                qk_chunk(w4, wk, c % 2, C("qg"), (qn_v[0:64, c, 0, :], qn_v[64:128, c, 1, :]), qn_k(c))
            if "qn" in taps and taps["qn"] == ti:
                tap("qn", qn_v[:, :, :, :], [k_ for c_ in range(4) for k_ in qn_k(c_)], [128, 4, 2, T], BF16)
                tap("kn", kring[:, :, kslot * T:(kslot + 1) * T], [f"kr{kslot}.{c_}" for c_ in range(4)], [128, 4, T], BF16)
                b0_ = (T0 // 128) % 20
                tap("vr", vring[:, b0_:b0_ + 4, :], [f"vr{b0_ + i_}" for i_ in range(4)], [128, 4, 768], BF16)
            deltas = [0] + [d for d in range(-384, 2049, 128) if d != 0]
            blocks = []
            for dl in deltas:
                s0 = T0 - dl
                if s0 < 0:
                    continue
                q0 = max(0, -dl)
                q1 = min(T, 2176 - dl)
                blocks.append((dl, s0, q0, q1))
            nb = len(blocks)
            pair_out = {}
            for hd in range(8):
                c, half = hd // 2, hd % 2
                pr = slice(half * 64, half * 64 + 64)
                ev, ek = acquire(f"e.{hd}")
                po, pok = bank("C")
                vcol = c * 192 + half * 64

                def qk(bi, hd=hd, c=c, pr=pr):
                    dl, s0, q0, q1 = blocks[bi]
                    pa, pak = bank("A" if (bi // 2) % 2 == 0 else "B")
                    ks = (s0 // T) % 5
                    kc0 = ks * T + (s0 % T)
                    S.op("pe", lambda e: e.matmul(pa[:, q0:q1], kring[:, c, kc0:kc0 + 128], qn_v[:, c, hd % 2, q0:q1], start=True, stop=True),
                         reads=[f"kr{ks}.{c}"] + qn_k(c), writes=[pak])
                    return pa, pak

                pend = [qk(0)] + ([qk(1)] if nb > 1 else [])
                for bi in range(nb):
                    dl, s0, q0, q1 = blocks[bi]
                    pa, pak = pend.pop(0)
                    if bi + 2 < nb:
                        pend.append(qk(bi + 2))
                    ex, exk = bs()
                    S.op("act", lambda e, ex=ex, pa=pa, q0=q0, q1=q1: e.activation(ex[:, q0:q1], pa[:, q0:q1], AF.Exp, scale=0.125),
                         reads=[pak], writes=[exk])
                    x0 = dl + 384 + q0
                    S.op("dve", lambda e, ex=ex, q0=q0, q1=q1, x0=x0, ev=ev: e.tensor_tensor(ex[:, q0:q1], ex[:, q0:q1], ev[:, x0:x0 + (q1 - q0)], op=ALU.mult),
                         reads=[exk, ek], writes=[exk])
                    blk = (s0 // 128) % 20
                    S.op("pe", lambda e, ex=ex, q0=q0, q1=q1, blk=blk, po=po, bi=bi, vcol=vcol: e.matmul(
                        po[:, q0:q1], vring[:, blk, vcol:vcol + 128], ex[:, q0:q1], start=(bi == 0), stop=(bi == nb - 1), skip_group_check=True),
                         reads=[exk, f"vr{blk}"], writes=[pok])
                pair_out[half] = (po, pok)
                if half == 1:
                    (pe_, pek), (po_, pok_) = pair_out[0], pair_out[1]
                    num, numk = fs()
                    den, denk = fs()
                    S.op("act", lambda e, num=num, pe_=pe_: e.copy(num[0:64, :], pe_[0:64, :]), reads=[pek], writes=[numk])
                    S.op("dve", lambda e, den=den, pe_=pe_: e.tensor_copy(den[64:128, :], pe_[64:128, :]), reads=[pek], writes=[denk])
                    S.op("act", lambda e, num=num, po_=po_: e.copy(num[64:128, :], po_[64:128, :]), reads=[pok_], writes=[numk])
                    S.op("dve", lambda e, den=den, po_=po_: e.tensor_copy(den[0:64, :], po_[0:64, :]), reads=[pok_], writes=[denk])
                    pm, pmk = bank("M")
                    S.op("pe", lambda e, den=den, pm=pm: e.matmul(pm[:, :], C("swp"), den[:, :], start=True, stop=True), reads=[denk, "cons"], writes=[pmk])
                    S.op("act", lambda e, den=den, pm=pm: e.activation(den[:, :], pm[:, :], AF.Ln), reads=[pmk], writes=[denk])
                    S.op("act", lambda e, den=den: e.activation(den[:, :], den[:, :], AF.Exp, scale=-1.0), reads=[denk], writes=[denk])
                    S.op("dve", lambda e, den=den, num=num, c=c: e.tensor_tensor(ymix[:, 4 + c, :], num[:, :], den[:, :], op=ALU.mult),
                         reads=[numk, denk], writes=[f"ymix.{4 + c}"])
            cw = C("hcw")
            gcs = []
            for c in range(4):
                wv, wk = acquire(f"gccx.{c}")
                w4 = wv[:, 0:2048].rearrange("p (o k m) -> p o k m", o=2, k=8)
                pa, pak = bank("A")
                pb_, pbk = bank("B")
                for kc in range(8):
                    S.op("pe", lambda e, kc=kc, w4=w4, pa=pa: e.matmul(pa[:, :], w4[:, 0, kc, :], hn[:, kc, :], start=(kc == 0), stop=(kc == 7)),
                         reads=[wk, hn_k(kc)], writes=[pak])
                for kc in range(8):
                    S.op("pe", lambda e, kc=kc, w4=w4, pb_=pb_: e.matmul(pb_[:, :], w4[:, 1, kc, :], hn[:, kc, :], start=(kc == 0), stop=(kc == 7)),
                         reads=[wk, hn_k(kc)], writes=[pbk])
                t1, t1k = fs()
                S.op("act", lambda e, t1=t1, pa=pa: e.copy(t1[:, :], pa[:, :]), reads=[pak], writes=[t1k])
                S.op("pool", lambda e, c=c: e.tensor_copy(ub_v[:, c, 0:2], uhalo[:, c, :]), reads=["uhalo"], writes=ub_k(c))
                S.op("dve", lambda e, t1=t1, pb_=pb_, c=c: e.tensor_tensor(ub_v[:, c, 2:2 + T], t1[:, :], pb_[:, :], op=ALU.mult),
                     reads=[t1k, pbk], writes=ub_k(c))
                S.op("pool", lambda e, c=c: e.tensor_copy(uhalo[:, c, :], ub_v[:, c, T:T + 2]), reads=ub_k(c), writes=["uhalo"])
                S.op("dve", lambda e, t1=t1, c=c: e.tensor_scalar(t1[:, :], ub_v[:, c, 2:2 + T], cw[:, c * 3 + 2:c * 3 + 3], None, op0=ALU.mult),
                     reads=ub_k(c) + ["cons"], writes=[t1k])
                S.op("dve", lambda e, t1=t1, c=c: e.scalar_tensor_tensor(t1[:, :], ub_v[:, c, 1:1 + T], cw[:, c * 3 + 1:c * 3 + 2], t1[:, :], op0=ALU.mult, op1=ALU.add),
                     reads=ub_k(c) + [t1k, "cons"], writes=[t1k])
                S.op("dve", lambda e, t1=t1, c=c: e.scalar_tensor_tensor(t1[:, :], ub_v[:, c, 0:T], cw[:, c * 3:c * 3 + 1], t1[:, :], op0=ALU.mult, op1=ALU.add),
                     reads=ub_k(c) + [t1k, "cons"], writes=[t1k])
                gcs.append((t1, t1k))
            for c in range(4):
                if c % 2 == 0:
                    wv, wk = acquire(f"gb.{c // 2}")
                    w4 = wv[:, 0:2048].rearrange("p (o k m) -> p o k m", o=2, k=8)
                pa, pak = bank("A")
                for kc in range(8):
                    S.op("pe", lambda e, kc=kc, w4=w4, pa=pa, c=c: e.matmul(pa[:, :], w4[:, c % 2, kc, :], hn[:, kc, :], start=(kc == 0), stop=(kc == 7)),
                         reads=[wk, hn_k(kc)], writes=[pak])
                t1, t1k = gcs[c]
                S.op("dve", lambda e, t1=t1, pa=pa, c=c: e.tensor_tensor(ymix[:, c, :], t1[:, :], pa[:, :], op=ALU.mult),
                     reads=[t1k, pak], writes=[f"ymix.{c}"])
            if "ymix0" in taps and taps["ymix0"] == ti:
                tap("ymix0", ymix[:, :, :], [f"ymix.{c}" for c in range(8)], [128, 8, T], BF16)
            proj_out(hb, "ho")

        def mixer1(hb, ti):
            h = hbuf[hb]
            rmsnorm(hb, "g.mix_norm.1")
            cw = C("rcw")
            cb = C("rcb")

            def proj(b):
                nh = 0 if b == 0 else 1
                xv_, xk = acquire(f"rx.{b}", hold=nh)
                yv_, yk = acquire(f"ry.{b}", hold=nh + 1)
                lv, lk = acquire(f"lru.{b}", hold=nh + 2)
                wx4 = xv_[:, 0:2048].rearrange("p (o k m) -> p o k m", o=2, k=8)
                wy4 = yv_[:, 0:2048].rearrange("p (o k m) -> p o k m", o=2, k=8)
                l5 = lv[:, 0:1024].rearrange("p (a j k m) -> p a j k m", a=2, j=2, k=2)
                xbv = xb2_v[b % 2]
                for j in range(2):
                    c = 2 * b + j
                    pa, pak = bank("A")
                    for kc in range(8):
                        S.op("pe", lambda e, kc=kc, pa=pa, j=j, wx4=wx4: e.matmul(pa[:, :], wx4[:, j, kc, :], hn[:, kc, :], start=(kc == 0), stop=(kc == 7)),
                             reads=[xk, hn_k(kc)], writes=[pak])
                    S.op("pool", lambda e, j=j, c=c, xbv=xbv: e.tensor_copy(xbv[:, j, 0:3], xhalo[:, c, :]), reads=[f"xhalo.{c}"], writes=xb2_k(b % 2, j))
                    S.op("act", lambda e, j=j, pa=pa, xbv=xbv: e.copy(xbv[:, j, 3:3 + T], pa[:, :]), reads=[pak], writes=xb2_k(b % 2, j))
                    S.op("pool", lambda e, j=j, c=c, xbv=xbv: e.tensor_copy(xhalo[:, c, :], xbv[:, j, T:T + 3]), reads=xb2_k(b % 2, j), writes=[f"xhalo.{c}"])
                gls = []
                for j in range(2):
                    pb_, pbk = bank("B")
                    for kc in range(8):
                        S.op("pe", lambda e, kc=kc, pb_=pb_, j=j, wy4=wy4: e.matmul(pb_[:, :], wy4[:, j, kc, :], hn[:, kc, :], start=(kc == 0), stop=(kc == 7)),
                             reads=[yk, hn_k(kc)], writes=[pbk])
                    gl, glk = GL[(b % 2) * 2 + j], f"gl{(b % 2) * 2 + j}"
                    S.op("act", lambda e, gl=gl, pb_=pb_: e.activation(gl[:, :], pb_[:, :], AF.Gelu_apprx_tanh), reads=[pbk], writes=[glk])
                    gls.append((gl, glk))
                return {"l5": l5, "lk": lk, "gls": gls, "xbv": xbv}

            def chain_a(b, stt):
                xbv = stt["xbv"]
                xc_v, xcb_v = xc2_v[b % 2], xcb2_v[b % 2]
                for j in range(2):
                    c = 2 * b + j
                    S.op("dve", lambda e, j=j, c=c: e.tensor_scalar(xc_v[:, j, :], xbv[:, j, 3:3 + T], cw[:, c * 4 + 3:c * 4 + 4], cb[:, c:c + 1], op0=ALU.mult, op1=ALU.add),
                         reads=xb2_k(b % 2, j) + ["cons"], writes=xc2_k(b % 2, j))
                    for tp in range(3):
                        S.op("dve", lambda e, j=j, c=c, tp=tp: e.scalar_tensor_tensor(xc_v[:, j, :], xbv[:, j, tp:tp + T], cw[:, c * 4 + tp:c * 4 + tp + 1], xc_v[:, j, :],
                                                                              op0=ALU.mult, op1=ALU.add),
                             reads=xb2_k(b % 2, j) + xc2_k(b % 2, j) + ["cons"], writes=xc2_k(b % 2, j))
                    S.op("act", lambda e, j=j: e.copy(xcb_v[:, j, :], xc_v[:, j, :]), reads=xc2_k(b % 2, j), writes=xcb2_k(b % 2, j))

            def chain_b(b, stt):
                l5, lk, gls = stt["l5"], stt["lk"], stt["gls"]
                xc_v, xcb_v = xc2_v[b % 2], xcb2_v[b % 2]
                pgs = []
                for j in range(2):
                    pga, pgak = (bank("M") if j == 0 else bank("C"))
                    pgx, pgxk = (bank("N") if j == 0 else bank("C"))
                    for kc in range(2):
                        S.op("pe", lambda e, kc=kc, j=j, pga=pga: e.matmul(pga[:, :], l5[:, 0, j, kc, :], xcb_v[:, kc, :], start=(kc == 0), stop=(kc == 1)),
                             reads=[lk] + xcb2_k(b % 2, kc), writes=[pgak])
                    for kc in range(2):
                        S.op("pe", lambda e, kc=kc, j=j, pgx=pgx: e.matmul(pgx[:, :], l5[:, 1, j, kc, :], xcb_v[:, kc, :], start=(kc == 0), stop=(kc == 1)),
                             reads=[lk] + xcb2_k(b % 2, kc), writes=[pgxk])
                    pgs.append((pga, pgak, pgx, pgxk))
                ch = []
                for j in range(2):
                    c = 2 * b + j
                    pga, pgak, pgx, pgxk = pgs[j]
                    ra, rak = fs()
                    ri, rik = fs()
                    S.op("act", lambda e, ra=ra, pga=pga, c=c: e.activation(ra[:, :], pga[:, :], AF.Tanh, scale=0.5, bias=hcon[:, c:c + 1]), reads=[pgak, "hcon"], writes=[rak])
                    S.op("act", lambda e, ri=ri, pgx=pgx, c=c: e.activation(ri[:, :], pgx[:, :], AF.Tanh, scale=0.5, bias=hcon[:, 8 + c:9 + c]), reads=[pgxk, "hcon"], writes=[rik])
                    S.op("act", lambda e, ra=ra, c=c: e.activation(ra[:, :], ra[:, :], AF.Exp, scale=hcon[:, 16 + c:17 + c], bias=hcon[:, 16 + c:17 + c]), reads=[rak, "hcon"], writes=[rak])
                    ch.append([ra, rak, ri, rik])
                for j in range(2):
                    ra, rak, ri, rik = ch[j]
                    mm_, mmk = fs()
                    S.op("dve", lambda e, mm_=mm_, ra=ra: e.tensor_tensor(mm_[:, :], ra[:, :], ra[:, :], op=ALU.mult), reads=[rak], writes=[mmk])
                    S.op("dve", lambda e, mm_=mm_: e.tensor_scalar(mm_[:, :], mm_[:, :], 1.0, None, op0=ALU.min), reads=[mmk], writes=[mmk])
                    S.op("dve", lambda e, ri=ri, j=j: e.scalar_tensor_tensor(ri[:, :], ri[:, :], 1.0, xc_v[:, j, :], op0=ALU.add, op1=ALU.mult), reads=[rik] + xc2_k(b % 2, j), writes=[rik])
                    ch[j] += [mm_, mmk]
                for j in range(2):
                    ra, rak, ri, rik, mm_, mmk = ch[j]
                    S.op("act", lambda e, mm_=mm_: e.activation(mm_[:, :], mm_[:, :], AF.Sqrt, scale=-0.25, bias=0.25), reads=[mmk], writes=[mmk])
                for j in range(2):
                    c = 2 * b + j
                    ra, rak, ri, rik, mm_, mmk = ch[j]
                    gl, glk = gls[j]
                    S.op("dve", lambda e, ri=ri, mm_=mm_: e.tensor_tensor(ri[:, :], ri[:, :], mm_[:, :], op=ALU.mult), reads=[rik, mmk], writes=[rik])
                    S.op("dve", lambda e, mm_=mm_, ra=ra, ri=ri, c=c: e.tensor_tensor_scan(mm_[:, :], ra[:, :], ri[:, :], carry[:, c:c + 1], op0=ALU.mult, op1=ALU.add),
                         reads=[rak, rik, f"carry.{c}"], writes=[mmk])
                    S.op("act", lambda e, mm_=mm_, c=c: e.copy(carry[:, c:c + 1], mm_[:, T - 1:T]), reads=[mmk], writes=[f"carry.{c}"])
                    S.op("dve", lambda e, gl=gl, mm_=mm_, c=c: e.tensor_tensor(ymix[:, c, :], mm_[:, :], gl[:, :], op=ALU.mult),
                         reads=[mmk, glk], writes=[f"ymix.{c}"])

            sts = {0: proj(0)}
            chain_a(0, sts[0])
            for b in range(4):
                if b < 3:
                    sts[b + 1] = proj(b + 1)
                    chain_a(b + 1, sts[b + 1])
                chain_b(b, sts[b])
            if "ymix1" in taps and taps["ymix1"] == ti:
                tap("ymix1", ymix[:, :, :], [f"ymix.{c}" for c in range(8)], [128, 8, T], BF16)
            proj_out(hb, "ro")

        def load_x(ti):
            hb = ti % 2
            S.op("act", lambda e: e.dma_start(out=hbuf[hb][:, :, :], in_=xv[:, :, ti * T:(ti + 1) * T]),
                 writes=[hk(hb, c) for c in range(8)], dma=f"x{hb}")

        def htap(name, hb, ti):
            if name in taps and taps[name] == ti:
                tap(name, hbuf[hb][:, :, :], [hk(hb, c) for c in range(8)], [128, 8, T], F32)

        load_x(0)
        for ti in range(NT):
            hb = ti % 2
            nsq_begin()
            for c_ in range(8):
                nsq_chunk(hb, c_)
            ffn(hb, 0, 1, "g.ffn1_norm.0")
            if ti + 1 < NT:
                load_x(ti + 1)
            htap("h_ffn1_0", hb, ti)
            mixer0(hb, ti)
            htap("h_mix_0", hb, ti)
            ffn(hb, 0, 2, "g.ffn2_norm.0")
            ple(hb, 0, ti)
            htap("h_l0", hb, ti)
            ffn(hb, 1, 1, "g.ffn1_norm.1")
            mixer1(hb, ti)
            htap("h_mix_1", hb, ti)
            ffn(hb, 1, 2, "g.ffn2_norm.1")
            ple(hb, 1, ti, nsq=False)
            tok = S.op("act", lambda e, hb=hb, ti=ti: e.dma_start(out=yv[:, :, ti * T:(ti + 1) * T], in_=hbuf[hb][:, :, :]),
                       reads=[hk(hb, c) for c in range(8)], writes=[f"yout{hb}"], dma=f"y{hb}")
            final_toks.append(tok)
        S.final_wait("act", final_toks)
        stats = S.emit(nc, st)
    return nc, stats, tap_out


def prep_common(inp):
    W = pack_weights(inp)
    G, M = attn_tables(np.asarray(inp["rel_bias"], np.float32))
    cons, names = small_consts(inp)
    names = dict(names)
    names["_n"] = cons.shape[1]
    return W, G, M, np.ascontiguousarray(cons), names


def kernel(**inputs):
    inp = {k: np.asarray(v) for k, v in inputs.items()}
    W, G, M, cons, names = prep_common(inp)
    x = inp["x"]
    p = inp["p"]
    B = x.shape[0]
    nc, stats, _ = build_program(SEQ, None, names)
    in_maps = []
    for b in range(B):
        in_maps.append({
            "xT": np.ascontiguousarray(x[b].T),
            "pT": np.ascontiguousarray(p[:, b].transpose(0, 2, 1)),
            "wsrc": W, "gtab": G, "mtab": M, "cons": cons,
        })
    res = run_bass_kernel_spmd(nc, in_maps, core_ids=list(range(B)))
    out = np.stack([np.ascontiguousarray(res.results[b]["yT"].T) for b in range(B)], axis=0)
    return out.astype(np.float32)
```
